# Optimizing a Trainium2 kernel written in Bass

```python
import math
import jax, jax.numpy as jnp
from jax import lax
import numpy as np

D_MODEL = 1024
BATCH = 32
SEQ = 256
DEPTH = 4
DEC_BATCH = 4
DEC_SEQ = 1024
PAST_LEN = 512

GRID_W = 64
N_HEADS_A = 4
QK_NOPE = 128
QK_ROPE = 64
V_HEAD = 128
Q_LORA = 384
KV_LORA = 256
WIDTH_A = N_HEADS_A * V_HEAD
ROPE_THETA = 10000.0
WIDTH_B = 256
N_HEADS_B = 4
HEAD_B = WIDTH_B // N_HEADS_B
CHUNK = 128
WIDTH_C = 256
CONV_W = 3
D_FF = 4 * D_MODEL
MIX_WIDTH = WIDTH_A + WIDTH_B + WIDTH_C
IN_SPLITS = (Q_LORA, KV_LORA, QK_ROPE, WIDTH_B, WIDTH_B, WIDTH_C, WIDTH_C, WIDTH_C)
IN_COLS = Q_LORA + KV_LORA + QK_ROPE + 2 * WIDTH_B + 3 * WIDTH_C
N_MOD = 6
EPS = 1e-6

kernel_name = "hybrid_diffusion_prefix_trunk_step"


def rmsnorm(x, g):
    xf = x.astype(jnp.float32)
    y = xf * lax.rsqrt(jnp.mean(xf * xf, axis=-1, keepdims=True) + EPS)
    return (y * g.astype(jnp.float32)).astype(x.dtype)


def axial_rope_tables(n_tokens):
    rows = n_tokens // GRID_W
    row = jnp.repeat(jnp.arange(rows, dtype=jnp.float32), GRID_W)
    col = jnp.tile(jnp.arange(GRID_W, dtype=jnp.float32), rows)
    nf = QK_ROPE // 4
    inv = ROPE_THETA ** (-jnp.arange(nf, dtype=jnp.float32) / nf)
    ang = jnp.stack([row[:, None] * inv, col[:, None] * inv], axis=1)
    return jnp.cos(ang), jnp.sin(ang)


def apply_axial_rope(x, cos, sin):
    xs = x.astype(jnp.float32).reshape(x.shape[:-1] + (2, 2, QK_ROPE // 4))
    x1, x2 = xs[..., 0, :], xs[..., 1, :]
    out = jnp.stack([x1 * cos - x2 * sin, x1 * sin + x2 * cos], axis=-2)
    return out.reshape(x.shape).astype(x.dtype)


def split_projection(z):
    idx, acc = [], 0
    for s in IN_SPLITS[:-1]:
        acc += s
        idx.append(acc)
    return jnp.split(z, idx, axis=-1)


def modulation(cond, w_ada, b_ada):
    m = jax.nn.silu(cond) @ w_ada + b_ada
    return jnp.split(m[:, None, :], N_MOD, axis=-1)


def mla_queries(q_c, g_q, w_uq):
    q = jnp.einsum('bnr,rhd->bhnd', rmsnorm(q_c, g_q), w_uq)
    return q[..., :QK_NOPE], q[..., QK_NOPE:]


def mla_keys_values(ckv, w_ukv):
    kv = jnp.einsum('blr,rhd->bhld', ckv, w_ukv)
    return kv[..., :QK_NOPE], kv[..., QK_NOPE:]


def mla_attend(q_nope, q_rope, k_nope, k_rope, v):
    b, _, n, _ = q_nope.shape
    scale = 1.0 / math.sqrt(QK_NOPE + QK_ROPE)
    s = (jnp.einsum('bhnd,bhld->bhnl', q_nope, k_nope)
         + jnp.einsum('bhnd,bld->bhnl', q_rope, k_rope)) * scale
    p = jax.nn.softmax(s.astype(jnp.float32), axis=-1).astype(v.dtype)
    o = jnp.einsum('bhnl,bhld->bnhd', p, v)
    return o.reshape(b, n, WIDTH_A)


def chunk_gmlp(u, v, g_v, w_s, b_s):
    b, n, _ = v.shape
    vn = rmsnorm(v, g_v).reshape(b, n // CHUNK, CHUNK, N_HEADS_B, HEAD_B)
    mixed = jnp.einsum('hpq,bcqhd->bcphd', w_s, vn) + b_s.T[None, None, :, :, None]
    return u * mixed.reshape(b, n, WIDTH_B)


def short_conv(bg, cg, hh, w_conv):
    n = hh.shape[1]
    z = cg * hh
    pad = CONV_W // 2
    zp = jnp.pad(z, ((0, 0), (pad, pad), (0, 0)))
    y = zp[:, 0:n] * w_conv[0]
    for k in range(1, CONV_W):
        y = y + zp[:, k:k + n] * w_conv[k]
    return bg * y


def trunk_layer(x, cond, ctx_ckv, ctx_krope, rope,
                w_ada, b_ada, g_pre_mix, w_in, g_q, w_uq, g_kv, w_ukv,
                g_v, w_s, b_s, w_conv, w_out, g_post_mix,
                g_pre_ffn, w_ff1, w_ff2, g_post_ffn):
    sh1, sc1, ga1, sh2, sc2, ga2 = modulation(cond, w_ada, b_ada)
    h = rmsnorm(x, g_pre_mix) * (1.0 + sc1) + sh1
    q_c, ckv_raw, kr, u, v, bg, cg, hh = split_projection(h @ w_in)
    u = jax.nn.gelu(u)
    v = jax.nn.gelu(v)
    ckv = rmsnorm(ckv_raw, g_kv)
    q_nope, q_rope = mla_queries(q_c, g_q, w_uq)
    k_nope, v_a = mla_keys_values(ckv, w_ukv)
    if ctx_ckv is None:
        o_a = mla_attend(q_nope, q_rope, k_nope, kr, v_a)
    else:
        cos, sin = rope
        q_rope = apply_axial_rope(q_rope, cos, sin)
        kr_lat = apply_axial_rope(kr, cos, sin)
        kc_nope, vc = mla_keys_values(ctx_ckv, w_ukv)
        o_a = mla_attend(q_nope, q_rope,
                         jnp.concatenate([kc_nope, k_nope], axis=2),
                         jnp.concatenate([ctx_krope, kr_lat], axis=1),
                         jnp.concatenate([vc, v_a], axis=2))
    o_b = chunk_gmlp(u, v, g_v, w_s, b_s)
    o_c = short_conv(bg, cg, hh, w_conv)
    mix = jnp.concatenate([o_a, o_b, o_c], axis=-1) @ w_out
    x = x + ga1 * rmsnorm(mix, g_post_mix)
    h2 = rmsnorm(x, g_pre_ffn) * (1.0 + sc2) + sh2
    f = jnp.square(jax.nn.relu(h2 @ w_ff1)) @ w_ff2
    x = x + ga2 * rmsnorm(f, g_post_ffn)
    return x, ckv, kr


def setup_inputs(seed: int = 0) -> dict:
    key = jax.random.key(seed)
    ks = jax.random.split(key, 24)
    f32 = jnp.float32

    def nrm(k, shape, scale):
        return jax.random.normal(k, shape, f32) * scale

    def gain(k, shape):
        return 1.0 + 0.02 * jax.random.normal(k, shape, f32)

    return {
        "x_prompt": nrm(ks[0], (BATCH, SEQ, D_MODEL), 1.0),
        "x_sample": nrm(ks[1], (DEC_BATCH, DEC_SEQ, D_MODEL), 1.0),
        "cache_ckv": nrm(ks[2], (DEC_BATCH, DEPTH, PAST_LEN, KV_LORA), 1.0),
        "cache_krope": nrm(ks[3], (DEC_BATCH, DEPTH, PAST_LEN, QK_ROPE), 1.0),
        "c": nrm(ks[4], (DEC_BATCH, D_MODEL), 1.0),
        "c_ctx": nrm(ks[5], (D_MODEL,), 1.0),
        "w_ada": nrm(ks[6], (DEPTH, D_MODEL, N_MOD * D_MODEL), 0.5 * D_MODEL ** -0.5),
        "b_ada": nrm(ks[7], (DEPTH, N_MOD * D_MODEL), 0.02),
        "g_pre_mix": gain(ks[8], (DEPTH, D_MODEL)),
        "w_in": nrm(ks[9], (DEPTH, D_MODEL, IN_COLS), D_MODEL ** -0.5),
        "g_q": gain(ks[10], (DEPTH, Q_LORA)),
        "w_uq": nrm(ks[11], (DEPTH, Q_LORA, N_HEADS_A, QK_NOPE + QK_ROPE), Q_LORA ** -0.5),
        "g_kv": gain(ks[12], (DEPTH, KV_LORA)),
        "w_ukv": nrm(ks[13], (DEPTH, KV_LORA, N_HEADS_A, QK_NOPE + V_HEAD), KV_LORA ** -0.5),
        "g_v": gain(ks[14], (DEPTH, WIDTH_B)),
        "w_s": nrm(ks[15], (DEPTH, N_HEADS_B, CHUNK, CHUNK), CHUNK ** -0.5),
        "b_s": 1.0 + nrm(ks[16], (DEPTH, N_HEADS_B, CHUNK), 0.02),
        "w_conv": nrm(ks[17], (DEPTH, CONV_W, WIDTH_C), CONV_W ** -0.5),
        "w_out": nrm(ks[18], (DEPTH, MIX_WIDTH, D_MODEL), MIX_WIDTH ** -0.5),
        "g_post_mix": gain(ks[19], (DEPTH, D_MODEL)),
        "g_pre_ffn": gain(ks[20], (DEPTH, D_MODEL)),
        "w_ff1": nrm(ks[21], (DEPTH, D_MODEL, D_FF), D_MODEL ** -0.5),
        "w_ff2": nrm(ks[22], (DEPTH, D_FF, D_MODEL), D_FF ** -0.5),
        "g_post_ffn": gain(ks[23], (DEPTH, D_MODEL)),
    }


def reference(x_prompt, x_sample, cache_ckv, cache_krope, c, c_ctx,
              w_ada, b_ada, g_pre_mix, w_in, g_q, w_uq, g_kv, w_ukv,
              g_v, w_s, b_s, w_conv, w_out, g_post_mix,
              g_pre_ffn, w_ff1, w_ff2, g_post_ffn):
    rope = axial_rope_tables(x_sample.shape[1])
    cond_ctx = c_ctx[None, :]
    xp, xs = x_prompt, x_sample
    ckv_list, kr_list = [], []
    for l in range(DEPTH):
        layer_w = (w_ada[l], b_ada[l], g_pre_mix[l], w_in[l], g_q[l], w_uq[l], g_kv[l], w_ukv[l],
                   g_v[l], w_s[l], b_s[l], w_conv[l], w_out[l], g_post_mix[l],
                   g_pre_ffn[l], w_ff1[l], w_ff2[l], g_post_ffn[l])
        xp, ckv_l, kr_l = trunk_layer(xp, cond_ctx, None, None, None, *layer_w)
        ckv_list.append(ckv_l)
        kr_list.append(kr_l)
        xs, _, _ = trunk_layer(xs, c, cache_ckv[:, l], cache_krope[:, l], rope, *layer_w)
    new_ckv = jnp.stack(ckv_list, axis=1)
    new_krope = jnp.stack(kr_list, axis=1)
    return (xp, xs, new_ckv, new_krope)
```

```python
import numpy as np
import concourse.bass as bass
import concourse.mybir as mybir
from concourse.bass_utils import run_bass_kernel_spmd

F32 = mybir.dt.float32
BF16 = mybir.dt.bfloat16
AF = mybir.ActivationFunctionType
ALU = mybir.AluOpType
AX = mybir.AxisListType

D = 1024
T = 1024
TBS = 512
DEPTH = 4
EPS = 1e-6
NSLOT = 4
SCALE = 1.0 / float(np.sqrt(192.0))
GELU = AF.Gelu_apprx_tanh
SAME_ENGINE_SYNC = True
ATTACH_WAITS = True

V_BADA, V_GPM, V_GPOM, V_GPF, V_GPOF, V_GQ, V_GKV, V_GV, V_WC, V_COND, V_MASK, NV = (
    0, 192, 224, 256, 288, 320, 332, 340, 348, 372, 388, 390)
PIECES_PER_LAYER = 36
ROPE_PERM = np.concatenate([np.arange(16, 32), np.arange(0, 16), np.arange(48, 64), np.arange(32, 48)])


class Res:
    __slots__ = ("name", "w", "r", "dsem", "dcnt", "excl", "regA")

    def __init__(self, name, excl=False, regA=False):
        self.name = name
        self.excl = excl
        self.regA = regA
        self.w = None
        self.r = {}
        self.dsem = None
        self.dcnt = 0


class Eng:
    def __init__(self, name, eng, semidx, is_pe=False):
        self.name = name
        self.eng = eng
        self.semidx = semidx
        self.is_pe = is_pe
        self.cnt = 0
        self.seen = {}


class KB:
    def __init__(self, nc):
        self.nc = nc
        self.sems = []
        self.dsems = {}
        self.pe = Eng("pe", nc.tensor, self.newsem("pe"), True)
        self.act = Eng("act", nc.scalar, self.newsem("act"))
        self.dve = Eng("dve", nc.vector, self.newsem("dve"))
        self.pool = Eng("pool", nc.gpsimd, self.newsem("pool"))
        self.sp = Eng("sp", nc.sync, None)
        self.compute = [self.pe, self.act, self.dve, self.pool]
        self.nops = 0
        self.region = {}
        self.inherit = {}

    def phase_switch(self):
        self.inherit = dict(self.region)

    def mkA(self, name):
        r = Res(name, regA=True)
        r.r = dict(self.inherit)
        return r

    def _note_region(self, ev, reads, writes):
        for x in list(reads) + list(writes):
            if x.regA:
                semidx, val, src = ev
                cur = self.region.get(semidx)
                if cur is None or cur[0] < val:
                    self.region[semidx] = (val, src)
                return

    def newsem(self, name=None):
        h = self.nc.alloc_semaphore(name or ("s%d" % len(self.sems)))
        self.sems.append(h)
        return len(self.sems) - 1

    def _waits(self, E, reads, writes, attach_last=False):
        need = {}

        def add(ev):
            semidx, val, src = ev
            if src is E and (E.is_pe or not SAME_ENGINE_SYNC):
                return
            if need.get(semidx, 0) < val:
                need[semidx] = val

        for r in reads:
            if r.w is not None:
                add(r.w)
        for w in writes:
            if w.w is not None:
                add(w.w)
            for semidx, (val, src) in w.r.items():
                add((semidx, val, src))
        todo = []
        for semidx, val in need.items():
            if E.seen.get(semidx, 0) >= val:
                continue
            todo.append((semidx, val))
            E.seen[semidx] = val
        attach = None
        if attach_last and todo:
            attach = todo.pop()
        for semidx, val in todo:
            E.eng.wait_ge(self.sems[semidx], val)
        return attach

    def _record(self, ev, reads, writes):
        semidx, val, src = ev
        for r in reads:
            cur = r.r.get(semidx)
            if cur is None or cur[0] < val:
                r.r[semidx] = (val, src)
        for w in writes:
            w.w = ev
            w.r = {}

    def op(self, E, fn, reads=(), writes=()):
        ex = [r for r in reads if r.excl]
        if ex:
            writes = list(writes) + ex
            reads = [r for r in reads if not r.excl]
        attach = self._waits(E, reads, writes, attach_last=(ATTACH_WAITS and E.is_pe))
        if attach is None:
            ins = fn(E.eng)
        elif E.is_pe:
            proxy = _PEProxy(E.eng, self.sems[attach[0]], attach[1])
            ins = fn(proxy)
            assert proxy.att is None
        else:
            ins = fn(E.eng)
            ins._wait_ge(self.sems[attach[0]], attach[1])
        E.cnt += 1
        ins.then_inc(self.sems[E.semidx], 1)
        self._record((E.semidx, E.cnt, E), reads, writes)
        self._note_region((E.semidx, E.cnt, E), reads, writes)
        self.nops += 1

    def dma(self, Q, out, in_, reads=(), writes=()):
        self._waits(Q, reads, writes)
        res = writes[0] if writes else reads[0]
        ent = self.dsems.get(res.name)
        if ent is None:
            ent = self.dsems[res.name] = [self.newsem("d_" + res.name), 0]
        ent[1] += 16
        res.dsem, res.dcnt = ent[0], ent[1]
        Q.eng.dma_start(out=out, in_=in_).then_inc(self.sems[res.dsem], 16)
        self._record((res.dsem, res.dcnt, None), reads, writes)
        self._note_region((res.dsem, res.dcnt, None), reads, writes)

    def barrier(self, resources=()):
        engs = self.compute + [self.sp]
        snap = {F: F.cnt for F in self.compute}
        dm = {}
        for r in resources:
            if r.dsem is not None and r.dcnt > 0:
                dm[r.dsem] = max(dm.get(r.dsem, 0), r.dcnt)
        for E in engs:
            for F in self.compute:
                if F is E or snap[F] == 0:
                    continue
                if E.seen.get(F.semidx, 0) < snap[F]:
                    E.eng.wait_ge(self.sems[F.semidx], snap[F])
                    E.seen[F.semidx] = snap[F]
            for semidx, val in dm.items():
                if E.seen.get(semidx, 0) < val:
                    E.eng.wait_ge(self.sems[semidx], val)
                    E.seen[semidx] = val


class _PEProxy:
    def __init__(self, eng, sem, val):
        self.eng = eng
        self.att = (sem, val)

    def matmul(self, *a, **k):
        ins = self.eng.matmul(*a, **k)
        if self.att is not None:
            ins._wait_ge(self.att[0], self.att[1])
            self.att = None
        return ins


class Rot:
    def __init__(self, items):
        self.items = items
        self.i = 0

    def next(self):
        it = self.items[self.i % len(self.items)]
        self.i += 1
        return it


def ACTF(out, in_, func, bias=None, scale=None):
    kw = {}
    if bias is not None:
        kw["bias"] = bias
    if scale is not None:
        kw["scale"] = scale
    return lambda e: e.activation(out=out, in_=in_, func=func, **kw)


def TT(out, in0, in1, op):
    return lambda e: e.tensor_tensor(out=out, in0=in0, in1=in1, op=op)


def STT(out, in0, scalar, in1, op0, op1):
    return lambda e: e.scalar_tensor_tensor(out=out, in0=in0, scalar=scalar, in1=in1, op0=op0, op1=op1)


def TS1(out, in0, s1, op0):
    return lambda e: e.tensor_scalar(out=out, in0=in0, scalar1=s1, scalar2=None, op0=op0)


def CP(out, in_):
    return lambda e: e.tensor_copy(out=out, in_=in_)


def RCP(out, in_):
    return lambda e: e.reciprocal(out=out, in_=in_)


def MMG(out_ps, lhs_list, rhs_list):
    def fn(pe):
        n = len(lhs_list)
        ins = None
        for i in range(n):
            ins = pe.matmul(out_ps, lhs_list[i], rhs_list[i], start=(i == 0), stop=(i == n - 1))
        return ins
    return fn


class _Stop(Exception):
    pass


def build_program(nlayers=DEPTH, do_s=True, stop=None):
    nc = bass.Bass("TRN2", target_bir_lowering=False)

    def chk(label):
        if stop is not None and label == stop:
            raise _Stop()
    dt_in = lambda name, shape: nc.dram_tensor(name, shape, F32, kind="ExternalInput").ap()
    dt_out = lambda name, shape: nc.dram_tensor(name, shape, F32, kind="ExternalOutput").ap()
    xp_d = dt_in("xp", [D, T])
    xs_d = dt_in("xs", [D, T])
    cckv_d = dt_in("cckv", [DEPTH, 256, 512])
    ckr_d = dt_in("ckr", [DEPTH, 64, 512])
    vecs_d = dt_in("vecs", [128, NV])
    wp_d = dt_in("wp", [DEPTH * PIECES_PER_LAYER, 128, 4096])
    ct_d = dt_in("ct", [64, T])
    st_d = dt_in("st", [64, T])
    wst_d = dt_in("wst", [DEPTH, 128, 512])
    bsb_d = dt_in("bsb", [DEPTH, 128, 256])
    yp_d = dt_out("yp", [D, T])
    ys_d = dt_out("ys", [D, TBS])
    ockv_d = dt_out("ockv", [DEPTH, 256, T])
    okr_d = dt_out("okr", [DEPTH, 64, T])

    off = [0]

    def alloc(nwords):
        o = off[0]
        off[0] += (nwords + 7) // 8 * 8
        return o

    O_X = alloc(8192)
    O_H = alloc(4096)
    O_RING = alloc(2048 * NSLOT)
    O_BIG = alloc(4096)
    O_VECS = alloc(NV)
    O_MOD = alloc(DEPTH * 2 * 48)
    O_DER = alloc(32)
    O_SC = alloc(8)
    O_ONES = alloc(64)
    O_EPS = alloc(8)
    O_RSTD = alloc(1024)
    O_SQ = alloc(768)
    O_TMP = alloc(1536)
    O_PT = alloc(768)
    O_RDEN = alloc(1024)
    O_ST = alloc(3072)
    O_WST = alloc(256)
    O_BSB = alloc(256)
    O_VSQ = alloc(256)
    O_VSS = alloc(8)
    O_CGS = alloc(1024)
    O_GV2 = alloc(1024)
    O_A = alloc(16384)
    NW = off[0]
    assert NW * 4 <= 212000, NW * 4

    arena = nc.alloc_sbuf_tensor("arena", [128, NW], F32)
    ps = nc.alloc_psum_tensor("ps", [128, 8, 512], F32)

    def fv(o, n):
        return arena[:, o:o + n]

    def bv(o, nwords):
        return arena[:, o:o + nwords].bitcast(BF16)

    xT = fv(O_X, 8192).rearrange("p (k t) -> p k t", k=8)
    hT = bv(O_H, 4096).rearrange("p (k t) -> p k t", k=8)
    big2 = fv(O_H, 4096).rearrange("p (k t) -> p k t", k=8)
    slots = [bv(O_RING + 2048 * i, 2048) for i in range(NSLOT)]
    big = fv(O_BIG, 4096).rearrange("p (k t) -> p k t", k=8)
    vecs = fv(O_VECS, NV)
    MOD = fv(O_MOD, DEPTH * 2 * 48).rearrange("p (l c j) -> p l c j", l=DEPTH, c=2)
    DER = fv(O_DER, 32).rearrange("p (a k) -> p a k", a=4)
    scT = bv(O_SC, 8).rearrange("p (k c) -> p k c", k=8)
    ones = bv(O_ONES, 64)
    epsT = fv(O_EPS, 8)
    rstds = [fv(O_RSTD + 512 * i, 512) for i in range(2)]
    sqs = [bv(O_SQ + 256 * i, 256) for i in range(3)]
    tmps = [fv(O_TMP + 512 * i, 512) for i in range(3)]
    pts = [bv(O_PT + 256 * i, 256) for i in range(3)]
    rdens = [fv(O_RDEN + 512 * i, 512) for i in range(2)]
    ckv_st = [fv(O_ST + 1024 * i, 1024).rearrange("p (j t) -> p j t", j=2) for i in range(2)]
    kr_st = [fv(O_ST + 2048 + 512 * i, 512) for i in range(2)]
    CT = fv(O_ST, 1024)
    ST = fv(O_ST + 1024, 1024)
    wsT = bv(O_WST, 256)
    bsb = fv(O_BSB, 256).rearrange("p (f q) -> p f q", f=2)
    gvs = [fv(O_CGS + 256 * i, 256) for i in range(4)] + [fv(O_GV2 + 256 * i, 256) for i in range(4)]
    vsq = fv(O_VSQ, 256)
    vss = fv(O_VSS, 8)
    cgss = [fv(O_CGS + 512 * i, 512) for i in range(2)]
    qTn = bv(O_A + 0, 2048).rearrange("p (h t) -> p h t", h=4)
    qTr = bv(O_A + 2048, 2048).rearrange("p (h t) -> p h t", h=4)
    knT = bv(O_A + 4096, 3072).rearrange("p (h t) -> p h t", h=4)
    va = bv(O_A + 7168, 3072).rearrange("p (c n) -> p c n", c=12)
    krTb = bv(O_A + 10240, 768)
    vn = bv(O_A + 11008, 1024).rearrange("p (c n) -> p c n", c=8)
    qnT = bv(O_A + 12032, 1536).rearrange("p (k t) -> p k t", k=3)
    ckvTb = bv(O_A + 13568, 1536).rearrange("p (j t) -> p j t", j=2)
    uT = bv(O_A + 2048, 1024).rearrange("p (j t) -> p j t", j=2)
    obT = bv(O_A + 3072, 1024).rearrange("p (j t) -> p j t", j=2)
    bgT = bv(O_A + 4096, 1024).rearrange("p (j t) -> p j t", j=2)
    ocT = bv(O_A + 5120, 1024).rearrange("p (j t) -> p j t", j=2)
    zpad = fv(O_A + 6144, 2064).rearrange("p (j t) -> p j t", j=2)
    ycv = fv(O_A + 8208, 2048).rearrange("p (j t) -> p j t", j=2)
    f1T = bv(O_A, 16384).rearrange("p (m t) -> p m t", m=32)

    K = KB(nc)
    pe, act, dve, pool, sp = K.pe, K.act, K.dve, K.pool, K.sp

    PS = [Res("ps%d" % i, excl=True) for i in range(8)]
    psb = [ps[:, i, :] for i in range(8)]
    wk = Rot([0, 1, 2, 3])
    wk_sets = {'m1': [0, 1, 2, 3, 4, 5, 7], 'm2': [0, 1, 2, 3], 'm3': [0, 1, 2, 3, 7], 'f': [0, 1, 2, 3]}
    xr = [[Res("x%d_%d" % (k, tb)) for tb in range(2)] for k in range(8)]
    hr = [[Res("h%d_%d" % (k, tb)) for tb in range(2)] for k in range(8)]
    slotr = [Res("slot%d" % i) for i in range(NSLOT)]
    bigr = [Res("big%d" % i) for i in range(8)]
    vecr = Res("vecs")
    modr = [[Res("mod%d_%d" % (l, p)) for p in range(2)] for l in range(DEPTH)]
    derr = [Res("der0"), Res("der1")]
    scr = Res("sc")
    constr = Res("const")
    rstdR = Rot([(rstds[i], Res("rstd%d" % i)) for i in range(2)])
    sqR = Rot([(sqs[i], Res("sq%d" % i)) for i in range(3)])
    tmpR = Rot([(tmps[i], Res("tmp%d" % i)) for i in range(3)])
    ptR = Rot([(pts[i], Res("pt%d" % i)) for i in range(3)])
    rdenR = Rot([(rdens[i], Res("rden%d" % i)) for i in range(2)])
    stckv = [[Res("st_ckv%d_%d" % (tb, j)) for j in range(2)] for tb in range(2)]
    stkr = [Res("st_kr0"), Res("st_kr1")]
    stR = [stckv[0][0], stckv[0][1], stckv[1][0], stckv[1][1], stkr[0], stkr[1]]
    ropeR = Res("rope")
    wsr = Res("wst")
    bsr = Res("bsb")
    halfr = [Res("half%d" % i) for i in range(8)]
    vsqr = Res("vsq")
    vssr = [Res("vss0"), Res("vss1")]
    cgsR = Rot([(cgss[i], (halfr[2 * i], halfr[2 * i + 1])) for i in range(2)])

    def sl(tb):
        return slice(tb * TBS, (tb + 1) * TBS)

    K.op(pool, lambda e: e.memset(ones, 1.0), writes=[constr])
    K.op(pool, lambda e: e.memset(epsT, EPS), writes=[constr])
    K.dma(sp, out=vecs, in_=vecs_d, writes=[vecr])

    seq = []
    seq += [(0 * PIECES_PER_LAYER + j, 4096) for j in range(6)]

    def stage_seq(l, is_s):
        b = l * PIECES_PER_LAYER
        first = [(b + 12, 4096), (b + 13, 4096), (b + 15, 3072), (b + 14, 2048)]
        mid = []
        if (not is_s) and l == 0:
            mid += [(j, 4096) for j in range(6, 12)]
        rest = [(b + 17, 4096), (b + 16, 4096), (b + 18, 4096), (b + 19, 4096)]
        for i in range(16):
            rest.append((b + 20 + i, 4096))
            if (not is_s) and l + 1 < nlayers and i < 12:
                rest.append(((l + 1) * PIECES_PER_LAYER + i, 4096))
        return first + mid + rest

    for l in range(nlayers):
        seq += stage_seq(l, False)
    if do_s:
        for l in range(nlayers):
            seq += stage_seq(l, True)

    ring = {"nload": 0, "nuse": 0}

    def ring_load():
        i = ring["nload"]
        if i >= len(seq):
            return
        idx, n = seq[i]
        s = i % NSLOT
        K.dma(pool, out=slots[s][:, 0:n], in_=wp_d[idx, :, 0:n], writes=[slotr[s]])
        ring["nload"] += 1

    def ring_get(expect_n):
        i = ring["nuse"]
        ring["nuse"] += 1
        assert i < ring["nload"], "ring underflow"
        assert seq[i][1] == expect_n, (i, seq[i], expect_n)
        return slots[i % NSLOT], slotr[i % NSLOT]

    for _ in range(NSLOT):
        ring_load()

    for c in range(2):
        K.op(act, ACTF(scT[:, :, c], vecs[:, V_COND + c * 8:V_COND + c * 8 + 8], AF.Silu), reads=[vecr], writes=[scr])

    modps = psb[7][:, 0:96].rearrange("p (j c) -> p j c", c=2)

    def emit_mod_piece(l, j):
        slot, sres = ring_get(4096)
        w = slot.rearrange("p (k n) -> p k n", k=8)

        def fn(e):
            ins = None
            for jj in range(4):
                ch = 4 * j + jj
                for k in range(8):
                    ins = e.matmul(modps[:, ch, :], w[:, k, jj * 128:(jj + 1) * 128], scT[:, k, :],
                                   start=(k == 0), stop=(k == 7))
            return ins
        K.op(pe, fn, reads=[sres, scr], writes=[PS[7]])
        ring_load()

    def emit_mod_evac(l, part):
        j0 = 24 * part
        for c in range(2):
            K.op(dve, TT(MOD[:, l, c, j0:j0 + 24], modps[:, j0:j0 + 24, c], vecs[:, V_BADA + l * 48 + j0:V_BADA + l * 48 + j0 + 24], ALU.add),
                 reads=[PS[7], vecr], writes=[modr[l][part]])

    def emit_mod(l, part):
        for j in range(6 * part, 6 * part + 6):
            emit_mod_piece(l, j)
        emit_mod_evac(l, part)

    def emit_derive(l, c, part):
        M = MOD[:, l, c, :]
        specs = [(0, 8, V_GPM, True), (1, 16, V_GPOM, False)] if part == 0 else [(2, 32, V_GPF, True), (3, 40, V_GPOF, False)]
        for a, sci, gi, plus1 in specs:
            g = vecs[:, gi + l * 8:gi + l * 8 + 8]
            if plus1:
                K.op(dve, STT(DER[:, a, :], M[:, sci:sci + 8], 1.0, g, ALU.add, ALU.mult), reads=[modr[l][part], vecr], writes=[derr[part]])
            else:
                K.op(dve, TT(DER[:, a, :], M[:, sci:sci + 8], g, ALU.mult), reads=[modr[l][part], vecr], writes=[derr[part]])

    def emit_rstd(n_feat):
        return emit_rstd_bank(6, n_feat)

    def emit_rstd_bank(bank, n_feat):
        rstd, rr = rstdR.next()
        K.op(act, ACTF(rstd, psb[bank], AF.Ln, bias=epsT[:, 0:1], scale=1.0 / n_feat), reads=[PS[bank], constr], writes=[rr])
        K.op(act, ACTF(rstd, rstd, AF.Exp, scale=-0.5), reads=[rr], writes=[rr])
        return rstd, rr

    def emit_prenorm(l, c, a_idx, b_off, tb):
        part = 0 if a_idx == 0 else 1
        for k in range(8):
            sq, sqr = sqR.next()
            if True:
                K.op(act, ACTF(sq, xT[:, k, sl(tb)], AF.Square), reads=[xr[k][tb]], writes=[sqr])
            else:
                K.op(pool, TT(sq, xT[:, k, sl(tb)], xT[:, k, sl(tb)], ALU.mult), reads=[xr[k][tb]], writes=[sqr])
            K.op(pe, (lambda sq, k: lambda e: e.matmul(psb[6], ones, sq, start=(k == 0), stop=(k == 7)))(sq, k),
                 reads=[sqr, constr], writes=[PS[6]])
        rstd, rr = emit_rstd(D)
        for k in range(8):
            tmp, tr = tmpR.next()
            K.op(dve, TT(tmp, xT[:, k, sl(tb)], rstd, ALU.mult),
                 reads=[xr[k][tb], rr], writes=[tr])
            K.op(act, ACTF(hT[:, k, sl(tb)], tmp, AF.Identity, bias=MOD[:, l, c, b_off + k:b_off + k + 1], scale=DER[:, a_idx, k:k + 1]),
                 reads=[tr, modr[l][part], derr[part]], writes=[hr[k][tb]])

    def flush_pend(pend, ssum_bank, keep=0):
        while len(pend) > keep:
            sq, sqr, first, last = pend.pop(0)
            K.op(pe, (lambda sq, first, last: lambda e: e.matmul(psb[ssum_bank], ones, sq, start=first, stop=last))(sq, first, last),
                 reads=[sqr, constr], writes=[PS[ssum_bank]])

    def emit_residual(src, sres_list, c_idx, rstd, rr, tb):
        for m in range(8):
            E = dve
            K.op(E, TT(src[:, m, :], src[:, m, :], rstd, ALU.mult),
                 reads=list(sres_list[m]) + [rr], writes=list(sres_list[m]))
            K.op(E, TT(xT[:, m, sl(tb)], xT[:, m, sl(tb)], src[:, m, :], ALU.add),
                 reads=list(sres_list[m]) + [xr[m][tb]], writes=[xr[m][tb]])

    def emit_pass(is_s, x_d, y_d):
        c = 1 if is_s else 0
        NSEQ, L = (2, 512) if is_s else (4, 256)
        KOFF = 512 if is_s else 0
        NKB = 3 if is_s else 2
        for k in range(8):
            K.dma(sp, out=xT[:, k, :], in_=x_d[k * 128:(k + 1) * 128, :], writes=[xr[k][0], xr[k][1]])
        if is_s:
            K.dma(sp, out=CT[0:64, :], in_=ct_d, writes=[ropeR] + stR)
            K.dma(sp, out=ST[0:64, :], in_=st_d, writes=[ropeR])

        for l in range(nlayers):
            nbq = 1 if (is_s and l == nlayers - 1) else 2
            qnr2 = [[K.mkA("qTn%d_%d" % (h, q)) for q in range(4)] for h in range(4)]
            qrr = [[K.mkA("qTr%d_%d" % (h, tb)) for tb in range(2)] for h in range(4)]
            knr = [[K.mkA("kn%d_%d" % (h, kb)) for kb in range(3)] for h in range(4)]
            var = [K.mkA("va%d" % i) for i in range(12)]
            krbr = [K.mkA("krb%d" % kb) for kb in range(3)]
            vnr = [K.mkA("vn%d" % i) for i in range(8)]
            qnr = [[K.mkA("qn%d_%d" % (j, tb)) for tb in range(2)] for j in range(3)]
            ckvbr = [[K.mkA("ckvb%d_%d" % (j, kb)) for kb in range(3)] for j in range(2)]
            regA = ([r for row in qnr2 for r in row] + [r for row in qrr for r in row] + [r for row in knr for r in row]
                    + var + krbr + vnr + [r for row in qnr for r in row] + [r for row in ckvbr for r in row])

            if (not is_s) and l == 0:
                emit_mod(0, 0)
            emit_derive(l, c, 0)

            K.dma(pool, out=wsT, in_=wst_d[l], writes=[wsr])
            K.dma(sp, out=bsb, in_=bsb_d[l].rearrange("p (f q) -> p f q", f=2), writes=[bsr])
            if is_s:
                for j in range(2):
                    K.dma(pool, out=ckvTb[:, j, 0:512], in_=cckv_d[l, j * 128:(j + 1) * 128, :], writes=[ckvbr[j][0]])
                K.dma(pool, out=krTb[0:64, 0:512], in_=ckr_d[l], writes=[krbr[0]])

            chk('mod')
            wk.items = wk_sets['m1']
            emit_prenorm(l, c, 0, 0, 0)
            chk('prenorm')

            slot, sres = ring_get(4096)
            wA = slot.rearrange("p (k n) -> p k n", k=8)
            for tb in range(2):
                if tb == 1:
                    emit_prenorm(l, c, 0, 0, 1)
                hs = [hT[:, k, sl(tb)] for k in range(8)]
                hres = [hr[k][tb] for k in range(8)]
                pendA = []
                for j in range(3 if tb < nbq else 0):
                    b = wk.next()
                    K.op(pe, MMG(psb[b], [wA[:, k, j * 128:(j + 1) * 128] for k in range(8)], hs),
                         reads=[sres] + hres, writes=[PS[b]])
                    flush_pend(pendA, 6)
                    K.op(dve, CP(big[:, j, :], psb[b]), reads=[PS[b]], writes=[bigr[j]])
                    sq, sqr = sqR.next()
                    K.op(act, ACTF(sq, big[:, j, :], AF.Square), reads=[bigr[j]], writes=[sqr])
                    pendA.append((sq, sqr, j == 0, j == 2))
                b = wk.next()
                K.op(pe, MMG(psb[b][0:64, :], [wA[:, k, 384:448] for k in range(8)], hs), reads=[sres] + hres, writes=[PS[b]])
                flush_pend(pendA, 6)
                if tb < nbq:
                    rstd, rr = emit_rstd(384)
                for j in range(3 if tb < nbq else 0):
                    K.op(dve, STT(qnT[:, j, sl(tb)], big[:, j, :], vecs[:, V_GQ + l * 3 + j:V_GQ + l * 3 + j + 1], rstd,
                                  ALU.mult, ALU.mult), reads=[bigr[j], rr, vecr], writes=[qnr[j][tb]])
                kdst = krTb[0:64, KOFF + tb * TBS:KOFF + (tb + 1) * TBS]
                kres = krbr[(KOFF // 512) + tb]
                if not is_s:
                    K.op(act, ACTF(kr_st[tb][0:64, :], psb[b][0:64, :], AF.Copy), reads=[PS[b]], writes=[stkr[tb]])
                    K.dma(sp, out=okr_d[l, :, sl(tb)], in_=kr_st[tb][0:64, :], reads=[stkr[tb]])
                    K.op(dve, CP(kdst, kr_st[tb][0:64, :]), reads=[stkr[tb]], writes=[kres])
                else:
                    b2 = wk.next()
                    K.op(pe, MMG(psb[b2][0:64, :], [wA[:, k, 448:512] for k in range(8)], hs), reads=[sres] + hres, writes=[PS[b2]])
                    t1, t1r = tmpR.next()
                    t2, t2r = tmpR.next()
                    K.op(dve, TT(t1[0:64, :], psb[b][0:64, :], CT[0:64, sl(tb)], ALU.mult), reads=[PS[b], ropeR], writes=[t1r])
                    K.op(dve, TT(t2[0:64, :], psb[b2][0:64, :], ST[0:64, sl(tb)], ALU.mult), reads=[PS[b2], ropeR], writes=[t2r])
                    K.op(pool, TT(kdst, t1[0:64, :], t2[0:64, :], ALU.add), reads=[t1r, t2r], writes=[kres])
            ring_load()

            chk('A')
            def emit_v_mm(tb, tc, wB, sres, hres):
                tok = slice(tb * TBS + tc * 128, tb * TBS + (tc + 1) * 128)
                b = wk.next()
                K.op(pe, MMG(psb[b][:, 0:256], [hT[:, k, tok] for k in range(8)], [wB[:, k, 256:512] for k in range(8)]),
                     reads=[sres] + hres, writes=[PS[b]])
                K.op(act, ACTF(gvs[tb * 4 + tc], psb[b][:, 0:256], GELU), reads=[PS[b]], writes=[halfr[tb * 4 + tc]])

            def emit_v_norm_a(tb):
                for tc in range(4):
                    gi = tb * 4 + tc
                    K.op(dve, TT(vsq, gvs[gi], gvs[gi], ALU.mult), reads=[halfr[gi]], writes=[vsqr])
                    K.op(dve, (lambda tc: lambda e: e.reduce_sum(out=vss[:, tb * 4 + tc:tb * 4 + tc + 1], in_=vsq, axis=AX.X))(tc), reads=[vsqr], writes=[vssr[tb]])

            def emit_v_norm_b(tb):
                K.op(act, ACTF(vss[:, tb * 4:tb * 4 + 4], vss[:, tb * 4:tb * 4 + 4], AF.Ln, bias=epsT[:, 0:1], scale=1.0 / 256), reads=[vssr[tb], constr], writes=[vssr[tb]])
                K.op(act, ACTF(vss[:, tb * 4:tb * 4 + 4], vss[:, tb * 4:tb * 4 + 4], AF.Exp, scale=-0.5), reads=[vssr[tb]], writes=[vssr[tb]])
                for tc in range(4):
                    gi = tb * 4 + tc
                    K.op(dve, TS1(vn[:, gi, :], gvs[gi], vss[:, gi:gi + 1], ALU.mult), reads=[halfr[gi], vssr[tb]], writes=[vnr[gi]])

            slot, sres = ring_get(4096)
            wB = slot.rearrange("p (k n) -> p k n", k=8)
            for tb in range(2):
                hs = [hT[:, k, sl(tb)] for k in range(8)]
                hres = [hr[k][tb] for k in range(8)]
                pendB = []
                for j in range(2):
                    b = wk.next()
                    K.op(pe, MMG(psb[b], [wB[:, k, j * 128:(j + 1) * 128] for k in range(8)], hs),
                         reads=[sres] + hres, writes=[PS[b]])
                    flush_pend(pendB, 6)
                    K.op(dve, CP(big[:, 4 + j, :], psb[b]), reads=[PS[b]], writes=[bigr[4 + j]])
                    sq, sqr = sqR.next()
                    K.op(act, ACTF(sq, big[:, 4 + j, :], AF.Square), reads=[bigr[4 + j]], writes=[sqr])
                    pendB.append((sq, sqr, j == 0, j == 1))
                kb = (KOFF // 512) + tb
                for tc in range(4 if tb < nbq else 0):
                    emit_v_mm(tb, tc, wB, sres, hres)
                flush_pend(pendB, 6)
                rstd, rr = emit_rstd(256)
                for j in range(2):
                    gsc = vecs[:, V_GKV + l * 2 + j:V_GKV + l * 2 + j + 1]
                    cdst = ckvTb[:, j, KOFF + tb * TBS:KOFF + (tb + 1) * TBS]
                    if not is_s:
                        K.op(dve, STT(ckv_st[tb][:, j, :], big[:, 4 + j, :], gsc, rstd, ALU.mult, ALU.mult),
                             reads=[bigr[4 + j], rr, vecr], writes=[stckv[tb][j]])
                        K.dma(sp, out=ockv_d[l, j * 128:(j + 1) * 128, sl(tb)], in_=ckv_st[tb][:, j, :], reads=[stckv[tb][j]])
                        K.op(act, ACTF(cdst, ckv_st[tb][:, j, :], AF.Copy), reads=[stckv[tb][j]], writes=[ckvbr[j][kb]])
                    else:
                        K.op(dve, STT(cdst, big[:, 4 + j, :], gsc, rstd, ALU.mult, ALU.mult),
                             reads=[bigr[4 + j], rr, vecr], writes=[ckvbr[j][kb]])
            ring_load()

            chk('U')
            K.op(pool, lambda e: e.memset(qTr[64:128, :, :], 0.0), writes=[r for row in qrr for r in row])
            K.op(pool, lambda e: e.memset(krTb[64:128, :], 0.0), writes=krbr)
            slot, sres = ring_get(3072)
            wQ = slot[:, 0:3072].rearrange("p (k n) -> p k n", k=3)
            for tb in range(nbq):
                qs = [qnT[:, kc, sl(tb)] for kc in range(3)]
                qres = [qnr[kc][tb] for kc in range(3)]
                for h in range(4):
                    b = wk.next()
                    K.op(pe, MMG(psb[b], [wQ[:, kc, h * 256:h * 256 + 128] for kc in range(3)], qs),
                         reads=[sres] + qres, writes=[PS[b]])
                    K.op(act, ACTF(qTn[:, h, sl(tb)], psb[b], AF.Copy), reads=[PS[b]], writes=[qnr2[h][2 * tb], qnr2[h][2 * tb + 1]])
                    b = wk.next()
                    K.op(pe, MMG(psb[b][0:64, :], [wQ[:, kc, h * 256 + 128:h * 256 + 192] for kc in range(3)], qs),
                         reads=[sres] + qres, writes=[PS[b]])
                    if not is_s:
                        K.op(dve, CP(qTr[0:64, h, sl(tb)], psb[b][0:64, :]), reads=[PS[b]], writes=[qrr[h][tb]])
                    else:
                        b2 = wk.next()
                        K.op(pe, MMG(psb[b2][0:64, :], [wQ[:, kc, h * 256 + 192:h * 256 + 256] for kc in range(3)], qs),
                             reads=[sres] + qres, writes=[PS[b2]])
                        t1, t1r = tmpR.next()
                        t2, t2r = tmpR.next()
                        K.op(dve, TT(t1[0:64, :], psb[b][0:64, :], CT[0:64, sl(tb)], ALU.mult), reads=[PS[b], ropeR], writes=[t1r])
                        K.op(dve, TT(t2[0:64, :], psb[b2][0:64, :], ST[0:64, sl(tb)], ALU.mult), reads=[PS[b2], ropeR], writes=[t2r])
                        K.op(pool, TT(qTr[0:64, h, sl(tb)], t1[0:64, :], t2[0:64, :], ALU.add), reads=[t1r, t2r], writes=[qrr[h][tb]])
            ring_load()

            chk('B')
            slot, sres = ring_get(2048)
            wU = slot[:, 0:2048].rearrange("p (k n) -> p k n", k=2)
            evac = Rot([act, dve])
            for h in range(4):
                for kb in range(NKB):
                    b = wk.next()
                    K.op(pe, MMG(psb[b], [wU[:, kc, h * 256:h * 256 + 128] for kc in range(2)],
                                 [ckvTb[:, kc, kb * 512:(kb + 1) * 512] for kc in range(2)]),
                         reads=[sres, ckvbr[0][kb], ckvbr[1][kb]], writes=[PS[b]])
                    E = evac.next()
                    dst = knT[:, h, kb * 512:(kb + 1) * 512]
                    if E is act:
                        K.op(act, ACTF(dst, psb[b], AF.Copy), reads=[PS[b]], writes=[knr[h][kb]])
                    else:
                        K.op(dve, CP(dst, psb[b]), reads=[PS[b]], writes=[knr[h][kb]])
            for kch in range(NKB * 4):
                kb = kch // 4
                b = wk.next()

                def fnv(e, b=b, kch=kch):
                    ins = None
                    for h in range(4):
                        for kc in range(2):
                            ins = e.matmul(psb[b][:, h * 128:(h + 1) * 128], ckvTb[:, kc, kch * 128:(kch + 1) * 128],
                                           wU[:, kc, h * 256 + 128:h * 256 + 256], start=(kc == 0), stop=(kc == 1))
                    return ins
                K.op(pe, fnv, reads=[sres, ckvbr[0][kb], ckvbr[1][kb]], writes=[PS[b]])
                E = evac.next()
                if E is act:
                    K.op(act, ACTF(va[:, kch, :], psb[b], AF.Copy), reads=[PS[b]], writes=[var[kch]])
                else:
                    K.op(dve, CP(va[:, kch, :], psb[b]), reads=[PS[b]], writes=[var[kch]])
            ring_load()

            chk('Q')
            wk.items = wk_sets['m2']
            if is_s:
                units = [(qb, h, slice(qb * 512, (qb + 1) * 512), list(range(12)), [2 * qb, 2 * qb + 1], qb) for qb in range(nbq) for h in range(4)]
            else:
                units = [(s, h, slice(s * 256, (s + 1) * 256), [2 * s, 2 * s + 1], [s], s // 2) for s in range(4) for h in range(4)]
            accR = Rot([(4, 6), (5, 7)])
            vnorm_at = {1: (emit_v_norm_a, 0), 3: (emit_v_norm_b, 0)}
            if nbq == 2:
                vnorm_at.update({4: (emit_v_norm_a, 1), 6: (emit_v_norm_b, 1)})
            def emit_scores_u(unit, kch):
                (_u0, h_, qsl_, _k, quarters_, tbq_) = unit
                nq_ = qsl_.stop - qsl_.start
                qres_ = [qnr2[h_][q] for q in quarters_] + [qrr[h_][tbq_]]
                b = wk.next()
                kb = kch // 4

                def fn(e):
                    e.matmul(psb[b][:, 0:nq_], knT[:, h_, kch * 128:(kch + 1) * 128], qTn[:, h_, qsl_], start=True, stop=False)
                    return e.matmul(psb[b][:, 0:nq_], krTb[:, kch * 128:(kch + 1) * 128], qTr[:, h_, qsl_], start=False, stop=True)
                K.op(pe, fn, reads=[knr[h_][kb], krbr[kb]] + qres_, writes=[PS[b]])
                return b

            pre = None
            for ui, unit in enumerate(units):
                (u0, h, qsl, kchs, quarters, tbq) = unit
                if ui in vnorm_at:
                    vnorm_at[ui][0](vnorm_at[ui][1])
                nq = qsl.stop - qsl.start
                ob, db = accR.next()
                if pre is not None:
                    bq = pre
                    pre = None
                else:
                    bq = [emit_scores_u(unit, kchs[0])]
                    if len(kchs) > 1:
                        bq.append(emit_scores_u(unit, kchs[1]))
                nxt = units[ui + 1] if (len(kchs) == 2 and ui + 1 < len(units)) else None
                prelist = []
                for i, kch in enumerate(kchs):
                    pt, ptr = ptR.next()
                    K.op(act, ACTF(pt[:, 0:nq], psb[bq[i]][:, 0:nq], AF.Exp, scale=SCALE), reads=[PS[bq[i]]], writes=[ptr])
                    if i + 2 < len(kchs):
                        bq.append(emit_scores_u(unit, kchs[i + 2]))
                    elif nxt is not None:
                        prelist.append(emit_scores_u(nxt, nxt[3][i]))
                    first, last = (i == 0), (i == len(kchs) - 1)

                    def fpv(e, pt=pt, kch=kch, first=first, last=last):
                        e.matmul(psb[ob][:, 0:nq], va[:, kch, h * 128:(h + 1) * 128], pt[:, 0:nq], start=first, stop=last)
                        return e.matmul(psb[db][:, 0:nq], ones, pt[:, 0:nq], start=first, stop=last)
                    K.op(pe, fpv, reads=[ptr, var[kch], constr], writes=[PS[ob], PS[db]])
                if nxt is not None:
                    pre = prelist
                rden, rdr = rdenR.next()
                K.op(act, ACTF(rden[:, 0:nq], psb[db][:, 0:nq], AF.Ln), reads=[PS[db]], writes=[rdr])
                K.op(act, ACTF(rden[:, 0:nq], rden[:, 0:nq], AF.Exp, scale=-1.0), reads=[rdr], writes=[rdr])
                K.op(dve, TT(qTn[:, h, qsl], psb[ob][:, 0:nq], rden[:, 0:nq], ALU.mult),
                     reads=[PS[ob], rdr], writes=[qnr2[h][q] for q in quarters])
            oar = qnr2

            chk('attn')
            if (not is_s) and l == 0:
                emit_mod(0, 1)
            emit_derive(l, c, 1)
            wk.items = wk_sets['m3']
            K.phase_switch()
            ur = [[K.mkA("u%d_%d" % (j, tb)) for tb in range(2)] for j in range(2)]
            obr = [[K.mkA("ob%d_%d" % (j, tb)) for tb in range(2)] for j in range(2)]
            bgr = [K.mkA("bg%d" % j) for j in range(2)]
            ocr = [K.mkA("oc%d" % j) for j in range(2)]
            zr = [K.mkA("z%d" % j) for j in range(2)]
            ycr = [K.mkA("yc%d" % j) for j in range(2)]
            K.op(pool, lambda e: e.memset(zpad, 0.0), writes=zr)

            slot, sres = ring_get(4096)
            wD = slot.rearrange("p (k n) -> p k n", k=8)
            for tb in range(2):
                hs = [hT[:, k, sl(tb)] for k in range(8)]
                hres = [hr[k][tb] for k in range(8)]
                for j in range(2):
                    b = wk.next()
                    K.op(pe, MMG(psb[b], [wD[:, k, j * 128:(j + 1) * 128] for k in range(8)], hs), reads=[sres] + hres, writes=[PS[b]])
                    cgs, cgr = cgsR.next()
                    K.op(act, ACTF(cgs, psb[b], AF.Copy), reads=[PS[b]], writes=list(cgr))
                    b2 = wk.next()
                    K.op(pe, MMG(psb[b2], [wD[:, k, 256 + j * 128:256 + (j + 1) * 128] for k in range(8)], hs), reads=[sres] + hres, writes=[PS[b2]])
                    if is_s:
                        zdst = zpad[:, j, tb * 514 + 1:tb * 514 + 513]
                        K.op(dve, TT(zdst, psb[b2], cgs, ALU.mult), reads=[PS[b2]] + list(cgr), writes=[zr[j]])
                    else:
                        zv = zpad[:, j, 0:1032].rearrange("p (s q) -> p s q", s=4)
                        zdst = zv[:, 2 * tb:2 * tb + 2, 1:257]
                        K.op(dve, TT(zdst, psb[b2].rearrange("p (s q) -> p s q", s=2), cgs.rearrange("p (s q) -> p s q", s=2), ALU.mult),
                             reads=[PS[b2]] + list(cgr), writes=[zr[j]])
            ring_load()
            nsq = NSEQ * nbq // 2
            for j in range(2):
                zv = zpad[:, j, 0:NSEQ * (L + 2)].rearrange("p (s q) -> p s q", s=NSEQ)
                yv = ycv[:, j, :].rearrange("p (s q) -> p s q", s=NSEQ)
                if is_s:
                    me = vecs[:, V_MASK:V_MASK + 1]
                    mo = vecs[:, V_MASK + 1:V_MASK + 2]
                    for (db, dc, sb_, sc_, mk) in ((0, 0, 1, 512, mo), (0, 513, 1, 1, me), (1, 0, 0, 512, me), (1, 513, 0, 1, mo)):
                        K.op(dve, TS1(zv[:, db, dc:dc + 1], zv[:, sb_, sc_:sc_ + 1], mk, ALU.mult), reads=[zr[j], vecr], writes=[zr[j]])
                wcs = [vecs[:, V_WC + l * 6 + tap * 2 + j:V_WC + l * 6 + tap * 2 + j + 1] for tap in range(3)]
                K.op(dve, TS1(yv[:, 0:nsq, :], zv[:, 0:nsq, 1:L + 1], wcs[1], ALU.mult), reads=[zr[j], vecr], writes=[ycr[j]])
                K.op(dve, STT(yv[:, 0:nsq, :], zv[:, 0:nsq, 0:L], wcs[0], yv[:, 0:nsq, :], ALU.mult, ALU.add), reads=[zr[j], vecr, ycr[j]], writes=[ycr[j]])
                K.op(dve, STT(yv[:, 0:nsq, :], zv[:, 0:nsq, 2:L + 2], wcs[2], yv[:, 0:nsq, :], ALU.mult, ALU.add), reads=[zr[j], vecr, ycr[j]], writes=[ycr[j]])

            slot, sres = ring_get(4096)
            wC = slot.rearrange("p (k n) -> p k n", k=8)
            for tb in range(nbq):
                hs = [hT[:, k, sl(tb)] for k in range(8)]
                hres = [hr[k][tb] for k in range(8)]
                for j in range(2):
                    b = wk.next()
                    K.op(pe, MMG(psb[b], [wC[:, k, j * 128:(j + 1) * 128] for k in range(8)], hs), reads=[sres] + hres, writes=[PS[b]])
                    K.op(act, ACTF(uT[:, j, sl(tb)], psb[b], GELU), reads=[PS[b]], writes=[ur[j][tb]])
                for j in range(2):
                    b = wk.next()
                    K.op(pe, MMG(psb[b], [wC[:, k, 256 + j * 128:256 + (j + 1) * 128] for k in range(8)], hs), reads=[sres] + hres, writes=[PS[b]])
                    K.op(dve, CP(bgT[:, j, sl(tb)], psb[b]), reads=[PS[b]], writes=[bgr[j]])
            ring_load()

            for j in range(2):
                K.op(pool, TT(ocT[:, j, 0:nbq * TBS], ycv[:, j, 0:nbq * TBS], bgT[:, j, 0:nbq * TBS], ALU.mult), reads=[ycr[j], bgr[j]], writes=[ocr[j]])
            for tci in range(4 * nbq):
                tb = tci // 4
                tok = slice(tci * 128, (tci + 1) * 128)
                for fc in range(2):
                    b = wk.next()
                    K.op(pe, (lambda b, tci, fc: lambda e: e.matmul(psb[b][:, 0:256], vn[:, tci, fc * 128:(fc + 1) * 128], wsT[:, fc * 256:(fc + 1) * 256], start=True, stop=True))(b, tci, fc),
                         reads=[vnr[tci], wsr], writes=[PS[b]])
                    tmp, tr = tmpR.next()
                    for hl in range(2):
                        pp = slice(hl * 64, (hl + 1) * 64)
                        K.op(dve, STT(tmp[pp, 0:128], psb[b][pp, hl * 128:(hl + 1) * 128], vecs[pp, V_GV + l * 2 + fc:V_GV + l * 2 + fc + 1],
                                      bsb[pp, fc, :], ALU.mult, ALU.add), reads=[PS[b], vecr, bsr], writes=[tr])
                    for hl in range(2):
                        pp = slice(hl * 64, (hl + 1) * 64)
                        K.op(pool, TT(obT[pp, fc, tok], tmp[pp, 0:128], uT[pp, fc, tok], ALU.mult), reads=[tr, ur[fc][tb]], writes=[obr[fc][tb]])

            chk('conv')
            slot0, sres0 = ring_get(4096)
            slot1, sres1 = ring_get(4096)
            wO = [slot0.rearrange("p (k n) -> p k n", k=8), slot1.rearrange("p (k n) -> p k n", k=8)]
            parkw = [big, big2]
            parkwres = [[[bigr[m]] for m in range(8)], [[hr[m][0], hr[m][1]] for m in range(8)]]
            for tb in range(nbq):
                rhs = [qTn[:, h, sl(tb)] for h in range(4)] + [obT[:, j, sl(tb)] for j in range(2)] + [ocT[:, j, sl(tb)] for j in range(2)]
                rres = [oar[h][2 * tb] for h in range(4)] + [oar[h][2 * tb + 1] for h in range(4)] + [obr[j][tb] for j in range(2)] + ocr
                pend = []
                for m in range(8):
                    b = wk.next()
                    w = wO[m // 4]
                    K.op(pe, MMG(psb[b], [w[:, k, (m % 4) * 128:(m % 4 + 1) * 128] for k in range(8)], rhs),
                         reads=[sres0, sres1] + rres, writes=[PS[b]])
                    flush_pend(pend, 4 + tb)
                    K.op(act, ACTF(parkw[tb][:, m, :], psb[b], AF.Identity, scale=DER[:, 1, m:m + 1]), reads=[PS[b], derr[0]], writes=parkwres[tb][m])
                    sq, sqr = sqR.next()
                    K.op(act, ACTF(sq, psb[b], AF.Square), reads=[PS[b]], writes=[sqr])
                    pend.append((sq, sqr, m == 0, m == 7))
                flush_pend(pend, 4 + tb)
                rstd, rr = emit_rstd_bank(4 + tb, D)
                emit_residual(parkw[tb], parkwres[tb], 1, rstd, rr, tb)
            ring_load()
            ring_load()
            chk('wout')

            wk.items = wk_sets['f']
            K.phase_switch()
            f1r = [[K.mkA("f1_%d_%d" % (m, tb)) for tb in range(2)] for m in range(32)]
            do_mod_next = (not is_s) and (l + 1 < nlayers)
            emit_prenorm(l, c, 2, 24, 0)
            for j in range(8):
                slot, sres = ring_get(4096)
                w1 = slot.rearrange("p (k n) -> p k n", k=8)
                for tb in range(nbq):
                    if j == 0 and tb == 1:
                        emit_prenorm(l, c, 2, 24, 1)
                    hs = [hT[:, k, sl(tb)] for k in range(8)]
                    hres = [hr[k][tb] for k in range(8)]
                    for mi in range(4):
                        m = 4 * j + mi
                        b = wk.next()
                        K.op(pe, MMG(psb[b], [w1[:, k, mi * 128:(mi + 1) * 128] for k in range(8)], hs), reads=[sres] + hres, writes=[PS[b]])
                        tmp, tr = tmpR.next()
                        K.op(act, ACTF(tmp, psb[b], AF.Relu), reads=[PS[b]], writes=[tr])
                        K.op(dve, TT(f1T[:, m, sl(tb)], tmp, tmp, ALU.mult), reads=[tr], writes=[f1r[m][tb]])
                ring_load()
                if do_mod_next:
                    emit_mod_piece(l + 1, j)
                    if j == 5:
                        emit_mod_evac(l + 1, 0)
            chk('ff1')
            park = [big, big2]
            parkres = [[[bigr[m]] for m in range(8)], [[hr[m][0], hr[m][1]] for m in range(8)]]
            pend = [[], []]

            def ff2_group(m, tb, w2, sres):
                b = wk.next()
                K.op(pe, MMG(psb[b], [w2[:, k, :] for k in range(32)], [f1T[:, k, sl(tb)] for k in range(32)]),
                     reads=[sres] + [f1r[k][tb] for k in range(32)], writes=[PS[b]])
                flush_pend(pend[tb], 4 + tb)
                K.op(act, ACTF(park[tb][:, m, :], psb[b], AF.Identity, scale=DER[:, 3, m:m + 1]), reads=[PS[b], derr[1]], writes=parkres[tb][m])
                sq, sqr = sqR.next()
                K.op(act, ACTF(sq, psb[b], AF.Square), reads=[PS[b]], writes=[sqr])
                pend[tb].append((sq, sqr, m == 0, m == 7))

            for m in range(6):
                slot, sres = ring_get(4096)
                w2 = slot.rearrange("p (k n) -> p k n", k=32)
                for tb in range(nbq):
                    ff2_group(m, tb, w2, sres)
                ring_load()
                if do_mod_next and m < 4:
                    emit_mod_piece(l + 1, 8 + m)
                    if m == 3:
                        emit_mod_evac(l + 1, 1)
            slot6, sres6 = ring_get(4096)
            slot7, sres7 = ring_get(4096)
            w26 = slot6.rearrange("p (k n) -> p k n", k=32)
            w27 = slot7.rearrange("p (k n) -> p k n", k=32)
            for tb in range(nbq):
                ff2_group(6, tb, w26, sres6)
                ff2_group(7, tb, w27, sres7)
                flush_pend(pend[tb], 4 + tb)
                rstd, rr = emit_rstd_bank(4 + tb, D)
                emit_residual(park[tb], parkres[tb], 3, rstd, rr, tb)
            ring_load()
            ring_load()
            K.phase_switch()
            chk('layer%d' % l)

        ncols = TBS if is_s else T
        for k in range(8):
            K.dma(sp, out=y_d[k * 128:(k + 1) * 128, :], in_=xT[:, k, 0:ncols], reads=[xr[k][0], xr[k][1]])

    try:
        emit_pass(False, xp_d, yp_d)
        if do_s:
            emit_pass(True, xs_d, ys_d)
    except _Stop:
        pass

    allres = [r for row in xr for r in row] + stR
    done = {}
    for r in allres:
        if r.dsem is not None:
            done[r.dsem] = max(done.get(r.dsem, 0), r.dcnt)
    for semidx, val in done.items():
        sp.eng.wait_ge(K.sems[semidx], val)
    return nc


def _fm(v):
    return np.ascontiguousarray(v.reshape(-1, 128).T)


def _piece_kn(w, c0, ncols):
    Kd = w.shape[0]
    blk = w[:, c0:c0 + ncols].reshape(Kd // 128, 128, ncols).transpose(1, 0, 2).reshape(128, -1)
    out = np.zeros((128, 4096), np.float32)
    out[:, :blk.shape[1]] = blk
    return out


def _rope_tables():
    rows = T // 64
    row = np.repeat(np.arange(rows, dtype=np.float32), 64)
    col = np.tile(np.arange(64, dtype=np.float32), rows)
    nf = 16
    inv = (np.float32(10000.0) ** (-np.arange(nf, dtype=np.float32) / np.float32(nf))).astype(np.float32)
    ang_r = (row[:, None] * inv).astype(np.float32)
    ang_c = (col[:, None] * inv).astype(np.float32)
    cr, sr, cc, sc = np.cos(ang_r), np.sin(ang_r), np.cos(ang_c), np.sin(ang_c)
    C = np.concatenate([cr, cr, cc, cc], axis=1)
    S = np.concatenate([-sr, sr, -sc, sc], axis=1)
    return np.ascontiguousarray(C.T.astype(np.float32)), np.ascontiguousarray(S.T.astype(np.float32))


_PROG = {}


def kernel(x_prompt, x_sample, cache_ckv, cache_krope, c, c_ctx,
           w_ada, b_ada, g_pre_mix, w_in, g_q, w_uq, g_kv, w_ukv,
           g_v, w_s, b_s, w_conv, w_out, g_post_mix,
           g_pre_ffn, w_ff1, w_ff2, g_post_ffn):
    f = lambda a: np.asarray(a, dtype=np.float32)
    x_prompt, x_sample, cache_ckv, cache_krope, c, c_ctx = map(f, (x_prompt, x_sample, cache_ckv, cache_krope, c, c_ctx))
    w_ada, b_ada, g_pre_mix, w_in, g_q, w_uq, g_kv, w_ukv = map(f, (w_ada, b_ada, g_pre_mix, w_in, g_q, w_uq, g_kv, w_ukv))
    g_v, w_s, b_s, w_conv, w_out, g_post_mix, g_pre_ffn, w_ff1, w_ff2, g_post_ffn = map(
        f, (g_v, w_s, b_s, w_conv, w_out, g_post_mix, g_pre_ffn, w_ff1, w_ff2, g_post_ffn))
    NC = 8
    wp = np.zeros((DEPTH * PIECES_PER_LAYER, 128, 4096), np.float32)
    for l in range(DEPTH):
        b = l * PIECES_PER_LAYER
        for j in range(12):
            wp[b + j] = _piece_kn(w_ada[l], j * 512, 512)
        wi = w_in[l]
        kr = wi[:, 640:704]
        A = np.concatenate([wi[:, 0:384], kr, kr[:, ROPE_PERM]], axis=1)
        B = np.concatenate([wi[:, 384:640], wi[:, 960:1216]], axis=1)
        Cc = np.concatenate([wi[:, 704:960], wi[:, 1216:1472]], axis=1)
        Dd = np.concatenate([wi[:, 1472:1728], wi[:, 1728:1984]], axis=1)
        wp[b + 12] = _piece_kn(A, 0, 512)
        wp[b + 13] = _piece_kn(B, 0, 512)
        wp[b + 14] = _piece_kn(w_ukv[l].reshape(256, 1024), 0, 1024)
        uq = w_uq[l]
        uqx = np.concatenate([uq[:, :, 0:128], uq[:, :, 128:192], uq[:, :, 128:192][:, :, ROPE_PERM]], axis=2).reshape(384, 1024)
        wp[b + 15] = _piece_kn(uqx, 0, 1024)
        wp[b + 16] = _piece_kn(Cc, 0, 512)
        wp[b + 17] = _piece_kn(Dd, 0, 512)
        wp[b + 18] = _piece_kn(w_out[l], 0, 512)
        wp[b + 19] = _piece_kn(w_out[l], 512, 512)
        for j in range(8):
            wp[b + 20 + j] = _piece_kn(w_ff1[l], j * 512, 512)
        for m in range(8):
            wp[b + 28 + m] = _piece_kn(w_ff2[l], m * 128, 128)
    vecs0 = np.zeros((128, NV), np.float32)
    for l in range(DEPTH):
        vecs0[:, V_BADA + l * 48:V_BADA + (l + 1) * 48] = _fm(b_ada[l])
        vecs0[:, V_GPM + l * 8:V_GPM + (l + 1) * 8] = _fm(g_pre_mix[l])
        vecs0[:, V_GPOM + l * 8:V_GPOM + (l + 1) * 8] = _fm(g_post_mix[l])
        vecs0[:, V_GPF + l * 8:V_GPF + (l + 1) * 8] = _fm(g_pre_ffn[l])
        vecs0[:, V_GPOF + l * 8:V_GPOF + (l + 1) * 8] = _fm(g_post_ffn[l])
        vecs0[:, V_GQ + l * 3:V_GQ + (l + 1) * 3] = _fm(g_q[l])
        vecs0[:, V_GKV + l * 2:V_GKV + (l + 1) * 2] = _fm(g_kv[l])
        vecs0[:, V_GV + l * 2:V_GV + (l + 1) * 2] = _fm(g_v[l])
        for tap in range(3):
            vecs0[:, V_WC + l * 6 + tap * 2:V_WC + l * 6 + tap * 2 + 2] = _fm(w_conv[l, tap])
    vecs0[:, V_COND:V_COND + 8] = _fm(c_ctx)
    wst = np.ascontiguousarray(w_s.transpose(0, 3, 1, 2).reshape(DEPTH, 128, 512))
    bsbh = np.zeros((DEPTH, 128, 2, 128), np.float32)
    for fc in range(2):
        for hl in range(2):
            bsbh[:, hl * 64:(hl + 1) * 64, fc, :] = b_s[:, fc * 2 + hl, None, :]
    bsbh = bsbh.reshape(DEPTH, 128, 256)
    ct, st = _rope_tables()

    in_maps = []
    for core in range(NC):
        bidx = core // 2
        v = vecs0.copy()
        v[:, V_COND + 8:V_COND + 16] = _fm(c[bidx])
        half = core % 2
        v[:, V_MASK] = 1.0 if half == 0 else 0.0
        v[:, V_MASK + 1] = 1.0 if half == 1 else 0.0
        perm = np.arange(T) if half == 0 else np.concatenate([np.arange(TBS, T), np.arange(0, TBS)])
        in_maps.append({
            "xp": np.ascontiguousarray(x_prompt[4 * core:4 * core + 4].reshape(T, D).T),
            "xs": np.ascontiguousarray(x_sample[bidx].T[:, perm]),
            "cckv": np.ascontiguousarray(cache_ckv[bidx].transpose(0, 2, 1)),
            "ckr": np.ascontiguousarray(cache_krope[bidx].transpose(0, 2, 1)),
            "vecs": v, "wp": wp, "ct": np.ascontiguousarray(ct[:, perm]), "st": np.ascontiguousarray(st[:, perm]), "wst": wst, "bsb": bsbh,
        })
    if "nc" not in _PROG:
        _PROG["nc"] = build_program()
    res = run_bass_kernel_spmd(_PROG["nc"], in_maps, core_ids=list(range(NC)))
    R = res.results
    y_prompt = np.zeros((32, 256, D), np.float32)
    y_sample = np.zeros((4, 1024, D), np.float32)
    new_ckv = np.zeros((32, DEPTH, 256, 256), np.float32)
    new_krope = np.zeros((32, DEPTH, 256, 64), np.float32)
    for core in range(NC):
        r = R[core]
        y_prompt[4 * core:4 * core + 4] = np.asarray(r["yp"]).T.reshape(4, 256, D)
        hf = core % 2
        y_sample[core // 2, hf * TBS:(hf + 1) * TBS] = np.asarray(r["ys"]).T
        ok = np.asarray(r["ockv"])
        new_ckv[4 * core:4 * core + 4] = ok.reshape(DEPTH, 256, 4, 256).transpose(2, 0, 3, 1)
        okr = np.asarray(r["okr"])
        new_krope[4 * core:4 * core + 4] = okr.reshape(DEPTH, 64, 4, 256).transpose(2, 0, 3, 1)
    return (y_prompt, y_sample, new_ckv, new_krope)
```

```python
import numpy as np
import concourse.bass as bass
import concourse.mybir as mybir
from concourse.bass_utils import run_bass_kernel_spmd

F32 = mybir.dt.float32
BF16 = mybir.dt.bfloat16
AF = mybir.ActivationFunctionType
ALU = mybir.AluOpType
AX = mybir.AxisListType

D = 1024
T = 1024
TBS = 512
DEPTH = 4
EPS = 1e-6
NSLOT = 4
SCALE = 1.0 / float(np.sqrt(192.0))
GELU = AF.Gelu_apprx_tanh
SAME_ENGINE_SYNC = True
ATTACH_WAITS = True

V_BADA, V_GPM, V_GPOM, V_GPF, V_GPOF, V_GQ, V_GKV, V_GV, V_WC, V_COND, V_MASK, NV = (
    0, 192, 224, 256, 288, 320, 332, 340, 348, 372, 388, 390)
PIECES_PER_LAYER = 36
ROPE_PERM = np.concatenate([np.arange(16, 32), np.arange(0, 16), np.arange(48, 64), np.arange(32, 48)])


class Res:
    __slots__ = ("name", "w", "r", "dsem", "dcnt", "excl", "regA")

    def __init__(self, name, excl=False, regA=False):
        self.name = name
        self.excl = excl
        self.regA = regA
        self.w = None
        self.r = {}
        self.dsem = None
        self.dcnt = 0


class Eng:
    def __init__(self, name, eng, semidx, is_pe=False):
        self.name = name
        self.eng = eng
        self.semidx = semidx
        self.is_pe = is_pe
        self.cnt = 0
        self.seen = {}


class KB:
    def __init__(self, nc):
        self.nc = nc
        self.sems = []
        self.dsems = {}
        self.pe = Eng("pe", nc.tensor, self.newsem("pe"), True)
        self.act = Eng("act", nc.scalar, self.newsem("act"))
        self.dve = Eng("dve", nc.vector, self.newsem("dve"))
        self.pool = Eng("pool", nc.gpsimd, self.newsem("pool"))
        self.sp = Eng("sp", nc.sync, None)
        self.compute = [self.pe, self.act, self.dve, self.pool]
        self.nops = 0
        self.region = {}
        self.inherit = {}

    def phase_switch(self):
        self.inherit = dict(self.region)

    def mkA(self, name):
        r = Res(name, regA=True)
        r.r = dict(self.inherit)
        return r

    def _note_region(self, ev, reads, writes):
        for x in list(reads) + list(writes):
            if x.regA:
                semidx, val, src = ev
                cur = self.region.get(semidx)
                if cur is None or cur[0] < val:
                    self.region[semidx] = (val, src)
                return

    def newsem(self, name=None):
        h = self.nc.alloc_semaphore(name or ("s%d" % len(self.sems)))
        self.sems.append(h)
        return len(self.sems) - 1

    def _waits(self, E, reads, writes, attach_last=False):
        need = {}

        def add(ev):
            semidx, val, src = ev
            if src is E and (E.is_pe or not SAME_ENGINE_SYNC):
                return
            if need.get(semidx, 0) < val:
                need[semidx] = val

        for r in reads:
            if r.w is not None:
                add(r.w)
        for w in writes:
            if w.w is not None:
                add(w.w)
            for semidx, (val, src) in w.r.items():
                add((semidx, val, src))
        todo = []
        for semidx, val in need.items():
            if E.seen.get(semidx, 0) >= val:
                continue
            todo.append((semidx, val))
            E.seen[semidx] = val
        attach = None
        if attach_last and todo:
            attach = todo.pop()
        for semidx, val in todo:
            E.eng.wait_ge(self.sems[semidx], val)
        return attach

    def _record(self, ev, reads, writes):
        semidx, val, src = ev
        for r in reads:
            cur = r.r.get(semidx)
            if cur is None or cur[0] < val:
                r.r[semidx] = (val, src)
        for w in writes:
            w.w = ev
            w.r = {}

    def op(self, E, fn, reads=(), writes=()):
        ex = [r for r in reads if r.excl]
        if ex:
            writes = list(writes) + ex
            reads = [r for r in reads if not r.excl]
        attach = self._waits(E, reads, writes, attach_last=(ATTACH_WAITS and E.is_pe))
        if attach is None:
            ins = fn(E.eng)
        elif E.is_pe:
            proxy = _PEProxy(E.eng, self.sems[attach[0]], attach[1])
            ins = fn(proxy)
            assert proxy.att is None
        else:
            ins = fn(E.eng)
            ins._wait_ge(self.sems[attach[0]], attach[1])
        E.cnt += 1
        ins.then_inc(self.sems[E.semidx], 1)
        self._record((E.semidx, E.cnt, E), reads, writes)
        self._note_region((E.semidx, E.cnt, E), reads, writes)
        self.nops += 1

    def dma(self, Q, out, in_, reads=(), writes=()):
        self._waits(Q, reads, writes)
        res = writes[0] if writes else reads[0]
        ent = self.dsems.get(res.name)
        if ent is None:
            ent = self.dsems[res.name] = [self.newsem("d_" + res.name), 0]
        ent[1] += 16
        res.dsem, res.dcnt = ent[0], ent[1]
        Q.eng.dma_start(out=out, in_=in_).then_inc(self.sems[res.dsem], 16)
        self._record((res.dsem, res.dcnt, None), reads, writes)
        self._note_region((res.dsem, res.dcnt, None), reads, writes)

    def barrier(self, resources=()):
        engs = self.compute + [self.sp]
        snap = {F: F.cnt for F in self.compute}
        dm = {}
        for r in resources:
            if r.dsem is not None and r.dcnt > 0:
                dm[r.dsem] = max(dm.get(r.dsem, 0), r.dcnt)
        for E in engs:
            for F in self.compute:
                if F is E or snap[F] == 0:
                    continue
                if E.seen.get(F.semidx, 0) < snap[F]:
                    E.eng.wait_ge(self.sems[F.semidx], snap[F])
                    E.seen[F.semidx] = snap[F]
            for semidx, val in dm.items():
                if E.seen.get(semidx, 0) < val:
                    E.eng.wait_ge(self.sems[semidx], val)
                    E.seen[semidx] = val


class _PEProxy:
    def __init__(self, eng, sem, val):
        self.eng = eng
        self.att = (sem, val)

    def matmul(self, *a, **k):
        ins = self.eng.matmul(*a, **k)
        if self.att is not None:
            ins._wait_ge(self.att[0], self.att[1])
            self.att = None
        return ins


class Rot:
    def __init__(self, items):
        self.items = items
        self.i = 0

    def next(self):
        it = self.items[self.i % len(self.items)]
        self.i += 1
        return it


def ACTF(out, in_, func, bias=None, scale=None):
    kw = {}
    if bias is not None:
        kw["bias"] = bias
    if scale is not None:
        kw["scale"] = scale
    return lambda e: e.activation(out=out, in_=in_, func=func, **kw)


def TT(out, in0, in1, op):
    return lambda e: e.tensor_tensor(out=out, in0=in0, in1=in1, op=op)


def STT(out, in0, scalar, in1, op0, op1):
    return lambda e: e.scalar_tensor_tensor(out=out, in0=in0, scalar=scalar, in1=in1, op0=op0, op1=op1)


def TS1(out, in0, s1, op0):
    return lambda e: e.tensor_scalar(out=out, in0=in0, scalar1=s1, scalar2=None, op0=op0)


def CP(out, in_):
    return lambda e: e.tensor_copy(out=out, in_=in_)


def RCP(out, in_):
    return lambda e: e.reciprocal(out=out, in_=in_)


def MMG(out_ps, lhs_list, rhs_list):
    def fn(pe):
        n = len(lhs_list)
        ins = None
        for i in range(n):
            ins = pe.matmul(out_ps, lhs_list[i], rhs_list[i], start=(i == 0), stop=(i == n - 1))
        return ins
    return fn


class _Stop(Exception):
    pass


def build_program(nlayers=DEPTH, do_s=True, stop=None):
    nc = bass.Bass("TRN2", target_bir_lowering=False)

    def chk(label):
        if stop is not None and label == stop:
            raise _Stop()
    dt_in = lambda name, shape: nc.dram_tensor(name, shape, F32, kind="ExternalInput").ap()
    dt_out = lambda name, shape: nc.dram_tensor(name, shape, F32, kind="ExternalOutput").ap()
    xp_d = dt_in("xp", [D, T])
    xs_d = dt_in("xs", [D, T])
    cckv_d = dt_in("cckv", [DEPTH, 256, 512])
    ckr_d = dt_in("ckr", [DEPTH, 64, 512])
    vecs_d = dt_in("vecs", [128, NV])
    wp_d = dt_in("wp", [DEPTH * PIECES_PER_LAYER, 128, 4096])
    ct_d = dt_in("ct", [64, T])
    st_d = dt_in("st", [64, T])
    wst_d = dt_in("wst", [DEPTH, 128, 512])
    bsb_d = dt_in("bsb", [DEPTH, 128, 256])
    yp_d = dt_out("yp", [D, T])
    ys_d = dt_out("ys", [D, TBS])
    ockv_d = dt_out("ockv", [DEPTH, 256, T])
    okr_d = dt_out("okr", [DEPTH, 64, T])

    off = [0]

    def alloc(nwords):
        o = off[0]
        off[0] += (nwords + 7) // 8 * 8
        return o

    O_X = alloc(8192)
    O_H = alloc(4096)
    O_RING = alloc(2048 * NSLOT)
    O_BIG = alloc(4096)
    O_VECS = alloc(NV)
    O_MOD = alloc(DEPTH * 2 * 48)
    O_DER = alloc(32)
    O_SC = alloc(8)
    O_ONES = alloc(64)
    O_EPS = alloc(8)
    O_RSTD = alloc(1024)
    O_SQ = alloc(768)
    O_TMP = alloc(1536)
    O_PT = alloc(768)
    O_RDEN = alloc(1024)
    O_ST = alloc(3072)
    O_WST = alloc(256)
    O_BSB = alloc(256)
    O_VSQ = alloc(256)
    O_VSS = alloc(8)
    O_CGS = alloc(1024)
    O_GV2 = alloc(1024)
    O_A = alloc(16384)
    NW = off[0]
    assert NW * 4 <= 212000, NW * 4

    arena = nc.alloc_sbuf_tensor("arena", [128, NW], F32)
    ps = nc.alloc_psum_tensor("ps", [128, 8, 512], F32)

    def fv(o, n):
        return arena[:, o:o + n]

    def bv(o, nwords):
        return arena[:, o:o + nwords].bitcast(BF16)

    xT = fv(O_X, 8192).rearrange("p (k t) -> p k t", k=8)
    hT = bv(O_H, 4096).rearrange("p (k t) -> p k t", k=8)
    big2 = fv(O_H, 4096).rearrange("p (k t) -> p k t", k=8)
    slots = [bv(O_RING + 2048 * i, 2048) for i in range(NSLOT)]
    big = fv(O_BIG, 4096).rearrange("p (k t) -> p k t", k=8)
    vecs = fv(O_VECS, NV)
    MOD = fv(O_MOD, DEPTH * 2 * 48).rearrange("p (l c j) -> p l c j", l=DEPTH, c=2)
    DER = fv(O_DER, 32).rearrange("p (a k) -> p a k", a=4)
    scT = bv(O_SC, 8).rearrange("p (k c) -> p k c", k=8)
    ones = bv(O_ONES, 64)
    epsT = fv(O_EPS, 8)
    rstds = [fv(O_RSTD + 512 * i, 512) for i in range(2)]
    sqs = [bv(O_SQ + 256 * i, 256) for i in range(3)]
    tmps = [fv(O_TMP + 512 * i, 512) for i in range(3)]
    pts = [bv(O_PT + 256 * i, 256) for i in range(3)]
    rdens = [fv(O_RDEN + 512 * i, 512) for i in range(2)]
    ckv_st = [fv(O_ST + 1024 * i, 1024).rearrange("p (j t) -> p j t", j=2) for i in range(2)]
    kr_st = [fv(O_ST + 2048 + 512 * i, 512) for i in range(2)]
    CT = fv(O_ST, 1024)
    ST = fv(O_ST + 1024, 1024)
    wsT = bv(O_WST, 256)
    bsb = fv(O_BSB, 256).rearrange("p (f q) -> p f q", f=2)
    gvs = [fv(O_CGS + 256 * i, 256) for i in range(4)] + [fv(O_GV2 + 256 * i, 256) for i in range(4)]
    vsq = fv(O_VSQ, 256)
    vss = fv(O_VSS, 8)
    cgss = [fv(O_CGS + 512 * i, 512) for i in range(2)]
    qTn = bv(O_A + 0, 2048).rearrange("p (h t) -> p h t", h=4)
    qTr = bv(O_A + 2048, 2048).rearrange("p (h t) -> p h t", h=4)
    knT = bv(O_A + 4096, 3072).rearrange("p (h t) -> p h t", h=4)
    va = bv(O_A + 7168, 3072).rearrange("p (c n) -> p c n", c=12)
    krTb = bv(O_A + 10240, 768)
    vn = bv(O_A + 11008, 1024).rearrange("p (c n) -> p c n", c=8)
    qnT = bv(O_A + 12032, 1536).rearrange("p (k t) -> p k t", k=3)
    ckvTb = bv(O_A + 13568, 1536).rearrange("p (j t) -> p j t", j=2)
    uT = bv(O_A + 2048, 1024).rearrange("p (j t) -> p j t", j=2)
    obT = bv(O_A + 3072, 1024).rearrange("p (j t) -> p j t", j=2)
    bgT = bv(O_A + 4096, 1024).rearrange("p (j t) -> p j t", j=2)
    ocT = bv(O_A + 5120, 1024).rearrange("p (j t) -> p j t", j=2)
    zpad = fv(O_A + 6144, 2064).rearrange("p (j t) -> p j t", j=2)
    ycv = fv(O_A + 8208, 2048).rearrange("p (j t) -> p j t", j=2)
    f1T = bv(O_A, 16384).rearrange("p (m t) -> p m t", m=32)

    K = KB(nc)
    pe, act, dve, pool, sp = K.pe, K.act, K.dve, K.pool, K.sp

    PS = [Res("ps%d" % i, excl=True) for i in range(8)]
    psb = [ps[:, i, :] for i in range(8)]
    wk = Rot([0, 1, 2, 3])
    wk_sets = {'m1': [0, 1, 2, 3, 4, 5, 7], 'm2': [0, 1, 2, 3], 'm3': [0, 1, 2, 3, 7], 'f': [0, 1, 2, 3]}
    xr = [[Res("x%d_%d" % (k, tb)) for tb in range(2)] for k in range(8)]
    hr = [[Res("h%d_%d" % (k, tb)) for tb in range(2)] for k in range(8)]
    slotr = [Res("slot%d" % i) for i in range(NSLOT)]
    bigr = [Res("big%d" % i) for i in range(8)]
    vecr = Res("vecs")
    modr = [[Res("mod%d_%d" % (l, p)) for p in range(2)] for l in range(DEPTH)]
    derr = [Res("der0"), Res("der1")]
    scr = Res("sc")
    constr = Res("const")
    rstdR = Rot([(rstds[i], Res("rstd%d" % i)) for i in range(2)])
    sqR = Rot([(sqs[i], Res("sq%d" % i)) for i in range(3)])
    tmpR = Rot([(tmps[i], Res("tmp%d" % i)) for i in range(3)])
    ptR = Rot([(pts[i], Res("pt%d" % i)) for i in range(3)])
    rdenR = Rot([(rdens[i], Res("rden%d" % i)) for i in range(2)])
    stckv = [[Res("st_ckv%d_%d" % (tb, j)) for j in range(2)] for tb in range(2)]
    stkr = [Res("st_kr0"), Res("st_kr1")]
    stR = [stckv[0][0], stckv[0][1], stckv[1][0], stckv[1][1], stkr[0], stkr[1]]
    ropeR = Res("rope")
    wsr = Res("wst")
    bsr = Res("bsb")
    halfr = [Res("half%d" % i) for i in range(8)]
    vsqr = Res("vsq")
    vssr = [Res("vss0"), Res("vss1")]
    cgsR = Rot([(cgss[i], (halfr[2 * i], halfr[2 * i + 1])) for i in range(2)])

    def sl(tb):
        return slice(tb * TBS, (tb + 1) * TBS)

    K.op(pool, lambda e: e.memset(ones, 1.0), writes=[constr])
    K.op(pool, lambda e: e.memset(epsT, EPS), writes=[constr])
    K.dma(sp, out=vecs, in_=vecs_d, writes=[vecr])

    seq = []
    seq += [(0 * PIECES_PER_LAYER + j, 4096) for j in range(6)]

    def stage_seq(l, is_s):
        b = l * PIECES_PER_LAYER
        first = [(b + 12, 4096), (b + 13, 4096), (b + 15, 3072), (b + 14, 2048)]
        mid = []
        if (not is_s) and l == 0:
            mid += [(j, 4096) for j in range(6, 12)]
        rest = [(b + 17, 4096), (b + 16, 4096), (b + 18, 4096), (b + 19, 4096)]
        for i in range(16):
            rest.append((b + 20 + i, 4096))
            if (not is_s) and l + 1 < nlayers and i < 12:
                rest.append(((l + 1) * PIECES_PER_LAYER + i, 4096))
        return first + mid + rest

    for l in range(nlayers):
        seq += stage_seq(l, False)
    if do_s:
        for l in range(nlayers):
            seq += stage_seq(l, True)

    ring = {"nload": 0, "nuse": 0}

    def ring_load():
        i = ring["nload"]
        if i >= len(seq):
            return
        idx, n = seq[i]
        s = i % NSLOT
        K.dma(pool, out=slots[s][:, 0:n], in_=wp_d[idx, :, 0:n], writes=[slotr[s]])
        ring["nload"] += 1

    def ring_get(expect_n):
        i = ring["nuse"]
        ring["nuse"] += 1
        assert i < ring["nload"], "ring underflow"
        assert seq[i][1] == expect_n, (i, seq[i], expect_n)
        return slots[i % NSLOT], slotr[i % NSLOT]

    for _ in range(NSLOT):
        ring_load()

    for c in range(2):
        K.op(act, ACTF(scT[:, :, c], vecs[:, V_COND + c * 8:V_COND + c * 8 + 8], AF.Silu), reads=[vecr], writes=[scr])

    modps = psb[7][:, 0:96].rearrange("p (j c) -> p j c", c=2)

    def emit_mod_piece(l, j):
        slot, sres = ring_get(4096)
        w = slot.rearrange("p (k n) -> p k n", k=8)

        def fn(e):
            ins = None
            for jj in range(4):
                ch = 4 * j + jj
                for k in range(8):
                    ins = e.matmul(modps[:, ch, :], w[:, k, jj * 128:(jj + 1) * 128], scT[:, k, :],
                                   start=(k == 0), stop=(k == 7))
            return ins
        K.op(pe, fn, reads=[sres, scr], writes=[PS[7]])
        ring_load()

    def emit_mod_evac(l, part):
        j0 = 24 * part
        for c in range(2):
            K.op(dve, TT(MOD[:, l, c, j0:j0 + 24], modps[:, j0:j0 + 24, c], vecs[:, V_BADA + l * 48 + j0:V_BADA + l * 48 + j0 + 24], ALU.add),
                 reads=[PS[7], vecr], writes=[modr[l][part]])

    def emit_mod(l, part):
        for j in range(6 * part, 6 * part + 6):
            emit_mod_piece(l, j)
        emit_mod_evac(l, part)

    def emit_derive(l, c, part):
        M = MOD[:, l, c, :]
        specs = [(0, 8, V_GPM, True), (1, 16, V_GPOM, False)] if part == 0 else [(2, 32, V_GPF, True), (3, 40, V_GPOF, False)]
        for a, sci, gi, plus1 in specs:
            g = vecs[:, gi + l * 8:gi + l * 8 + 8]
            if plus1:
                K.op(dve, STT(DER[:, a, :], M[:, sci:sci + 8], 1.0, g, ALU.add, ALU.mult), reads=[modr[l][part], vecr], writes=[derr[part]])
            else:
                K.op(dve, TT(DER[:, a, :], M[:, sci:sci + 8], g, ALU.mult), reads=[modr[l][part], vecr], writes=[derr[part]])

    def emit_rstd(n_feat):
        return emit_rstd_bank(6, n_feat)

    def emit_rstd_bank(bank, n_feat):
        rstd, rr = rstdR.next()
        K.op(act, ACTF(rstd, psb[bank], AF.Ln, bias=epsT[:, 0:1], scale=1.0 / n_feat), reads=[PS[bank], constr], writes=[rr])
        K.op(act, ACTF(rstd, rstd, AF.Exp, scale=-0.5), reads=[rr], writes=[rr])
        return rstd, rr

    def emit_prenorm(l, c, a_idx, b_off, tb):
        part = 0 if a_idx == 0 else 1
        for k in range(8):
            sq, sqr = sqR.next()
            if True:
                K.op(act, ACTF(sq, xT[:, k, sl(tb)], AF.Square), reads=[xr[k][tb]], writes=[sqr])
            else:
                K.op(pool, TT(sq, xT[:, k, sl(tb)], xT[:, k, sl(tb)], ALU.mult), reads=[xr[k][tb]], writes=[sqr])
            K.op(pe, (lambda sq, k: lambda e: e.matmul(psb[6], ones, sq, start=(k == 0), stop=(k == 7)))(sq, k),
                 reads=[sqr, constr], writes=[PS[6]])
        rstd, rr = emit_rstd(D)
        for k in range(8):
            tmp, tr = tmpR.next()
            K.op(dve, TT(tmp, xT[:, k, sl(tb)], rstd, ALU.mult),
                 reads=[xr[k][tb], rr], writes=[tr])
            K.op(act, ACTF(hT[:, k, sl(tb)], tmp, AF.Identity, bias=MOD[:, l, c, b_off + k:b_off + k + 1], scale=DER[:, a_idx, k:k + 1]),
                 reads=[tr, modr[l][part], derr[part]], writes=[hr[k][tb]])

    def flush_pend(pend, ssum_bank, keep=0):
        while len(pend) > keep:
            sq, sqr, first, last = pend.pop(0)
            K.op(pe, (lambda sq, first, last: lambda e: e.matmul(psb[ssum_bank], ones, sq, start=first, stop=last))(sq, first, last),
                 reads=[sqr, constr], writes=[PS[ssum_bank]])

    def emit_residual(src, sres_list, c_idx, rstd, rr, tb):
        for m in range(8):
            E = dve
            K.op(E, TT(src[:, m, :], src[:, m, :], rstd, ALU.mult),
                 reads=list(sres_list[m]) + [rr], writes=list(sres_list[m]))
            K.op(E, TT(xT[:, m, sl(tb)], xT[:, m, sl(tb)], src[:, m, :], ALU.add),
                 reads=list(sres_list[m]) + [xr[m][tb]], writes=[xr[m][tb]])

    def emit_pass(is_s, x_d, y_d):
        c = 1 if is_s else 0
        NSEQ, L = (2, 512) if is_s else (4, 256)
        KOFF = 512 if is_s else 0
        NKB = 3 if is_s else 2
        for k in range(8):
            K.dma(sp, out=xT[:, k, :], in_=x_d[k * 128:(k + 1) * 128, :], writes=[xr[k][0], xr[k][1]])
        if is_s:
            K.dma(sp, out=CT[0:64, :], in_=ct_d, writes=[ropeR] + stR)
            K.dma(sp, out=ST[0:64, :], in_=st_d, writes=[ropeR])

        for l in range(nlayers):
            nbq = 1 if (is_s and l == nlayers - 1) else 2
            qnr2 = [[K.mkA("qTn%d_%d" % (h, q)) for q in range(4)] for h in range(4)]
            qrr = [[K.mkA("qTr%d_%d" % (h, tb)) for tb in range(2)] for h in range(4)]
            knr = [[K.mkA("kn%d_%d" % (h, kb)) for kb in range(3)] for h in range(4)]
            var = [K.mkA("va%d" % i) for i in range(12)]
            krbr = [K.mkA("krb%d" % kb) for kb in range(3)]
            vnr = [K.mkA("vn%d" % i) for i in range(8)]
            qnr = [[K.mkA("qn%d_%d" % (j, tb)) for tb in range(2)] for j in range(3)]
            ckvbr = [[K.mkA("ckvb%d_%d" % (j, kb)) for kb in range(3)] for j in range(2)]
            regA = ([r for row in qnr2 for r in row] + [r for row in qrr for r in row] + [r for row in knr for r in row]
                    + var + krbr + vnr + [r for row in qnr for r in row] + [r for row in ckvbr for r in row])

            if (not is_s) and l == 0:
                emit_mod(0, 0)
            emit_derive(l, c, 0)

            K.dma(pool, out=wsT, in_=wst_d[l], writes=[wsr])
            K.dma(sp, out=bsb, in_=bsb_d[l].rearrange("p (f q) -> p f q", f=2), writes=[bsr])
            if is_s:
                for j in range(2):
                    K.dma(pool, out=ckvTb[:, j, 0:512], in_=cckv_d[l, j * 128:(j + 1) * 128, :], writes=[ckvbr[j][0]])
                K.dma(pool, out=krTb[0:64, 0:512], in_=ckr_d[l], writes=[krbr[0]])

            chk('mod')
            wk.items = wk_sets['m1']
            emit_prenorm(l, c, 0, 0, 0)
            chk('prenorm')

            slot, sres = ring_get(4096)
            wA = slot.rearrange("p (k n) -> p k n", k=8)
            for tb in range(2):
                if tb == 1:
                    emit_prenorm(l, c, 0, 0, 1)
                hs = [hT[:, k, sl(tb)] for k in range(8)]
                hres = [hr[k][tb] for k in range(8)]
                pendA = []
                for j in range(3 if tb < nbq else 0):
                    b = wk.next()
                    K.op(pe, MMG(psb[b], [wA[:, k, j * 128:(j + 1) * 128] for k in range(8)], hs),
                         reads=[sres] + hres, writes=[PS[b]])
                    flush_pend(pendA, 6)
                    K.op(dve, CP(big[:, j, :], psb[b]), reads=[PS[b]], writes=[bigr[j]])
                    sq, sqr = sqR.next()
                    K.op(act, ACTF(sq, big[:, j, :], AF.Square), reads=[bigr[j]], writes=[sqr])
                    pendA.append((sq, sqr, j == 0, j == 2))
                b = wk.next()
                K.op(pe, MMG(psb[b][0:64, :], [wA[:, k, 384:448] for k in range(8)], hs), reads=[sres] + hres, writes=[PS[b]])
                flush_pend(pendA, 6)
                if tb < nbq:
                    rstd, rr = emit_rstd(384)
                for j in range(3 if tb < nbq else 0):
                    K.op(dve, STT(qnT[:, j, sl(tb)], big[:, j, :], vecs[:, V_GQ + l * 3 + j:V_GQ + l * 3 + j + 1], rstd,
                                  ALU.mult, ALU.mult), reads=[bigr[j], rr, vecr], writes=[qnr[j][tb]])
                kdst = krTb[0:64, KOFF + tb * TBS:KOFF + (tb + 1) * TBS]
                kres = krbr[(KOFF // 512) + tb]
                if not is_s:
                    K.op(act, ACTF(kr_st[tb][0:64, :], psb[b][0:64, :], AF.Copy), reads=[PS[b]], writes=[stkr[tb]])
                    K.dma(sp, out=okr_d[l, :, sl(tb)], in_=kr_st[tb][0:64, :], reads=[stkr[tb]])
                    K.op(dve, CP(kdst, kr_st[tb][0:64, :]), reads=[stkr[tb]], writes=[kres])
                else:
                    b2 = wk.next()
                    K.op(pe, MMG(psb[b2][0:64, :], [wA[:, k, 448:512] for k in range(8)], hs), reads=[sres] + hres, writes=[PS[b2]])
                    t1, t1r = tmpR.next()
                    t2, t2r = tmpR.next()
                    K.op(dve, TT(t1[0:64, :], psb[b][0:64, :], CT[0:64, sl(tb)], ALU.mult), reads=[PS[b], ropeR], writes=[t1r])
                    K.op(dve, TT(t2[0:64, :], psb[b2][0:64, :], ST[0:64, sl(tb)], ALU.mult), reads=[PS[b2], ropeR], writes=[t2r])
                    K.op(pool, TT(kdst, t1[0:64, :], t2[0:64, :], ALU.add), reads=[t1r, t2r], writes=[kres])
            ring_load()

            chk('A')
            def emit_v_mm(tb, tc, wB, sres, hres):
                tok = slice(tb * TBS + tc * 128, tb * TBS + (tc + 1) * 128)
                b = wk.next()
                K.op(pe, MMG(psb[b][:, 0:256], [hT[:, k, tok] for k in range(8)], [wB[:, k, 256:512] for k in range(8)]),
                     reads=[sres] + hres, writes=[PS[b]])
                K.op(act, ACTF(gvs[tb * 4 + tc], psb[b][:, 0:256], GELU), reads=[PS[b]], writes=[halfr[tb * 4 + tc]])

            def emit_v_norm_a(tb):
                for tc in range(4):
                    gi = tb * 4 + tc
                    K.op(dve, TT(vsq, gvs[gi], gvs[gi], ALU.mult), reads=[halfr[gi]], writes=[vsqr])
                    K.op(dve, (lambda tc: lambda e: e.reduce_sum(out=vss[:, tb * 4 + tc:tb * 4 + tc + 1], in_=vsq, axis=AX.X))(tc), reads=[vsqr], writes=[vssr[tb]])

            def emit_v_norm_b(tb):
                K.op(act, ACTF(vss[:, tb * 4:tb * 4 + 4], vss[:, tb * 4:tb * 4 + 4], AF.Ln, bias=epsT[:, 0:1], scale=1.0 / 256), reads=[vssr[tb], constr], writes=[vssr[tb]])
                K.op(act, ACTF(vss[:, tb * 4:tb * 4 + 4], vss[:, tb * 4:tb * 4 + 4], AF.Exp, scale=-0.5), reads=[vssr[tb]], writes=[vssr[tb]])
                for tc in range(4):
                    gi = tb * 4 + tc
                    K.op(dve, TS1(vn[:, gi, :], gvs[gi], vss[:, gi:gi + 1], ALU.mult), reads=[halfr[gi], vssr[tb]], writes=[vnr[gi]])

            slot, sres = ring_get(4096)
            wB = slot.rearrange("p (k n) -> p k n", k=8)
            for tb in range(2):
                hs = [hT[:, k, sl(tb)] for k in range(8)]
                hres = [hr[k][tb] for k in range(8)]
                pendB = []
                for j in range(2):
                    b = wk.next()
                    K.op(pe, MMG(psb[b], [wB[:, k, j * 128:(j + 1) * 128] for k in range(8)], hs),
                         reads=[sres] + hres, writes=[PS[b]])
                    flush_pend(pendB, 6)
                    K.op(dve, CP(big[:, 4 + j, :], psb[b]), reads=[PS[b]], writes=[bigr[4 + j]])
                    sq, sqr = sqR.next()
                    K.op(act, ACTF(sq, big[:, 4 + j, :], AF.Square), reads=[bigr[4 + j]], writes=[sqr])
                    pendB.append((sq, sqr, j == 0, j == 1))
                kb = (KOFF // 512) + tb
                for tc in range(4 if tb < nbq else 0):
                    emit_v_mm(tb, tc, wB, sres, hres)
                flush_pend(pendB, 6)
                rstd, rr = emit_rstd(256)
                for j in range(2):
                    gsc = vecs[:, V_GKV + l * 2 + j:V_GKV + l * 2 + j + 1]
                    cdst = ckvTb[:, j, KOFF + tb * TBS:KOFF + (tb + 1) * TBS]
                    if not is_s:
                        K.op(dve, STT(ckv_st[tb][:, j, :], big[:, 4 + j, :], gsc, rstd, ALU.mult, ALU.mult),
                             reads=[bigr[4 + j], rr, vecr], writes=[stckv[tb][j]])
                        K.dma(sp, out=ockv_d[l, j * 128:(j + 1) * 128, sl(tb)], in_=ckv_st[tb][:, j, :], reads=[stckv[tb][j]])
                        K.op(act, ACTF(cdst, ckv_st[tb][:, j, :], AF.Copy), reads=[stckv[tb][j]], writes=[ckvbr[j][kb]])
                    else:
                        K.op(dve, STT(cdst, big[:, 4 + j, :], gsc, rstd, ALU.mult, ALU.mult),
                             reads=[bigr[4 + j], rr, vecr], writes=[ckvbr[j][kb]])
            ring_load()

            chk('U')
            K.op(pool, lambda e: e.memset(qTr[64:128, :, :], 0.0), writes=[r for row in qrr for r in row])
            K.op(pool, lambda e: e.memset(krTb[64:128, :], 0.0), writes=krbr)
            slot, sres = ring_get(3072)
            wQ = slot[:, 0:3072].rearrange("p (k n) -> p k n", k=3)
            for tb in range(nbq):
                qs = [qnT[:, kc, sl(tb)] for kc in range(3)]
                qres = [qnr[kc][tb] for kc in range(3)]
                for h in range(4):
                    b = wk.next()
                    K.op(pe, MMG(psb[b], [wQ[:, kc, h * 256:h * 256 + 128] for kc in range(3)], qs),
                         reads=[sres] + qres, writes=[PS[b]])
                    K.op(act, ACTF(qTn[:, h, sl(tb)], psb[b], AF.Copy), reads=[PS[b]], writes=[qnr2[h][2 * tb], qnr2[h][2 * tb + 1]])
                    b = wk.next()
                    K.op(pe, MMG(psb[b][0:64, :], [wQ[:, kc, h * 256 + 128:h * 256 + 192] for kc in range(3)], qs),
                         reads=[sres] + qres, writes=[PS[b]])
                    if not is_s:
                        K.op(dve, CP(qTr[0:64, h, sl(tb)], psb[b][0:64, :]), reads=[PS[b]], writes=[qrr[h][tb]])
                    else:
                        b2 = wk.next()
                        K.op(pe, MMG(psb[b2][0:64, :], [wQ[:, kc, h * 256 + 192:h * 256 + 256] for kc in range(3)], qs),
                             reads=[sres] + qres, writes=[PS[b2]])
                        t1, t1r = tmpR.next()
                        t2, t2r = tmpR.next()
                        K.op(dve, TT(t1[0:64, :], psb[b][0:64, :], CT[0:64, sl(tb)], ALU.mult), reads=[PS[b], ropeR], writes=[t1r])
                        K.op(dve, TT(t2[0:64, :], psb[b2][0:64, :], ST[0:64, sl(tb)], ALU.mult), reads=[PS[b2], ropeR], writes=[t2r])
                        K.op(pool, TT(qTr[0:64, h, sl(tb)], t1[0:64, :], t2[0:64, :], ALU.add), reads=[t1r, t2r], writes=[qrr[h][tb]])
            ring_load()

            chk('B')
            slot, sres = ring_get(2048)
            wU = slot[:, 0:2048].rearrange("p (k n) -> p k n", k=2)
            evac = Rot([act, dve])
            for h in range(4):
                for kb in range(NKB):
                    b = wk.next()
                    K.op(pe, MMG(psb[b], [wU[:, kc, h * 256:h * 256 + 128] for kc in range(2)],
                                 [ckvTb[:, kc, kb * 512:(kb + 1) * 512] for kc in range(2)]),
                         reads=[sres, ckvbr[0][kb], ckvbr[1][kb]], writes=[PS[b]])
                    E = evac.next()
                    dst = knT[:, h, kb * 512:(kb + 1) * 512]
                    if E is act:
                        K.op(act, ACTF(dst, psb[b], AF.Copy), reads=[PS[b]], writes=[knr[h][kb]])
                    else:
                        K.op(dve, CP(dst, psb[b]), reads=[PS[b]], writes=[knr[h][kb]])
            for kch in range(NKB * 4):
                kb = kch // 4
                b = wk.next()

                def fnv(e, b=b, kch=kch):
                    ins = None
                    for h in range(4):
                        for kc in range(2):
                            ins = e.matmul(psb[b][:, h * 128:(h + 1) * 128], ckvTb[:, kc, kch * 128:(kch + 1) * 128],
                                           wU[:, kc, h * 256 + 128:h * 256 + 256], start=(kc == 0), stop=(kc == 1))
                    return ins
                K.op(pe, fnv, reads=[sres, ckvbr[0][kb], ckvbr[1][kb]], writes=[PS[b]])
                E = evac.next()
                if E is act:
                    K.op(act, ACTF(va[:, kch, :], psb[b], AF.Copy), reads=[PS[b]], writes=[var[kch]])
                else:
                    K.op(dve, CP(va[:, kch, :], psb[b]), reads=[PS[b]], writes=[var[kch]])
            ring_load()

            chk('Q')
            wk.items = wk_sets['m2']
            if is_s:
                units = [(qb, h, slice(qb * 512, (qb + 1) * 512), list(range(12)), [2 * qb, 2 * qb + 1], qb) for qb in range(nbq) for h in range(4)]
            else:
                units = [(s, h, slice(s * 256, (s + 1) * 256), [2 * s, 2 * s + 1], [s], s // 2) for s in range(4) for h in range(4)]
            accR = Rot([(4, 6), (5, 7)])
            vnorm_at = {1: (emit_v_norm_a, 0), 3: (emit_v_norm_b, 0)}
            if nbq == 2:
                vnorm_at.update({4: (emit_v_norm_a, 1), 6: (emit_v_norm_b, 1)})
            for ui, (u0, h, qsl, kchs, quarters, tbq) in enumerate(units):
                if ui in vnorm_at:
                    vnorm_at[ui][0](vnorm_at[ui][1])
                nq = qsl.stop - qsl.start
                ob, db = accR.next()
                qres = [qnr2[h][q] for q in quarters] + [qrr[h][tbq]]

                def emit_scores(kch):
                    b = wk.next()
                    kb = kch // 4

                    def fn(e, b=b, kch=kch):
                        e.matmul(psb[b][:, 0:nq], knT[:, h, kch * 128:(kch + 1) * 128], qTn[:, h, qsl], start=True, stop=False)
                        return e.matmul(psb[b][:, 0:nq], krTb[:, kch * 128:(kch + 1) * 128], qTr[:, h, qsl], start=False, stop=True)
                    K.op(pe, fn, reads=[knr[h][kb], krbr[kb]] + qres, writes=[PS[b]])
                    return b
                bq = [emit_scores(kchs[0])]
                if len(kchs) > 1:
                    bq.append(emit_scores(kchs[1]))
                for i, kch in enumerate(kchs):
                    pt, ptr = ptR.next()
                    K.op(act, ACTF(pt[:, 0:nq], psb[bq[i]][:, 0:nq], AF.Exp, scale=SCALE), reads=[PS[bq[i]]], writes=[ptr])
                    if i + 2 < len(kchs):
                        bq.append(emit_scores(kchs[i + 2]))
                    first, last = (i == 0), (i == len(kchs) - 1)

                    def fpv(e, pt=pt, kch=kch, first=first, last=last):
                        e.matmul(psb[ob][:, 0:nq], va[:, kch, h * 128:(h + 1) * 128], pt[:, 0:nq], start=first, stop=last)
                        return e.matmul(psb[db][:, 0:nq], ones, pt[:, 0:nq], start=first, stop=last)
                    K.op(pe, fpv, reads=[ptr, var[kch], constr], writes=[PS[ob], PS[db]])
                rden, rdr = rdenR.next()
                K.op(act, ACTF(rden[:, 0:nq], psb[db][:, 0:nq], AF.Ln), reads=[PS[db]], writes=[rdr])
                K.op(act, ACTF(rden[:, 0:nq], rden[:, 0:nq], AF.Exp, scale=-1.0), reads=[rdr], writes=[rdr])
                K.op(dve, TT(qTn[:, h, qsl], psb[ob][:, 0:nq], rden[:, 0:nq], ALU.mult),
                     reads=[PS[ob], rdr], writes=[qnr2[h][q] for q in quarters])
            oar = qnr2

            chk('attn')
            if (not is_s) and l == 0:
                emit_mod(0, 1)
            emit_derive(l, c, 1)
            wk.items = wk_sets['m3']
            K.phase_switch()
            ur = [[K.mkA("u%d_%d" % (j, tb)) for tb in range(2)] for j in range(2)]
            obr = [[K.mkA("ob%d_%d" % (j, tb)) for tb in range(2)] for j in range(2)]
            bgr = [K.mkA("bg%d" % j) for j in range(2)]
            ocr = [K.mkA("oc%d" % j) for j in range(2)]
            zr = [K.mkA("z%d" % j) for j in range(2)]
            ycr = [K.mkA("yc%d" % j) for j in range(2)]
            K.op(pool, lambda e: e.memset(zpad, 0.0), writes=zr)

            slot, sres = ring_get(4096)
            wD = slot.rearrange("p (k n) -> p k n", k=8)
            for tb in range(2):
                hs = [hT[:, k, sl(tb)] for k in range(8)]
                hres = [hr[k][tb] for k in range(8)]
                for j in range(2):
                    b = wk.next()
                    K.op(pe, MMG(psb[b], [wD[:, k, j * 128:(j + 1) * 128] for k in range(8)], hs), reads=[sres] + hres, writes=[PS[b]])
                    cgs, cgr = cgsR.next()
                    K.op(act, ACTF(cgs, psb[b], AF.Copy), reads=[PS[b]], writes=list(cgr))
                    b2 = wk.next()
                    K.op(pe, MMG(psb[b2], [wD[:, k, 256 + j * 128:256 + (j + 1) * 128] for k in range(8)], hs), reads=[sres] + hres, writes=[PS[b2]])
                    if is_s:
                        zdst = zpad[:, j, tb * 514 + 1:tb * 514 + 513]
                        K.op(dve, TT(zdst, psb[b2], cgs, ALU.mult), reads=[PS[b2]] + list(cgr), writes=[zr[j]])
                    else:
                        zv = zpad[:, j, 0:1032].rearrange("p (s q) -> p s q", s=4)
                        zdst = zv[:, 2 * tb:2 * tb + 2, 1:257]
                        K.op(dve, TT(zdst, psb[b2].rearrange("p (s q) -> p s q", s=2), cgs.rearrange("p (s q) -> p s q", s=2), ALU.mult),
                             reads=[PS[b2]] + list(cgr), writes=[zr[j]])
            ring_load()
            nsq = NSEQ * nbq // 2
            for j in range(2):
                zv = zpad[:, j, 0:NSEQ * (L + 2)].rearrange("p (s q) -> p s q", s=NSEQ)
                yv = ycv[:, j, :].rearrange("p (s q) -> p s q", s=NSEQ)
                if is_s:
                    me = vecs[:, V_MASK:V_MASK + 1]
                    mo = vecs[:, V_MASK + 1:V_MASK + 2]
                    for (db, dc, sb_, sc_, mk) in ((0, 0, 1, 512, mo), (0, 513, 1, 1, me), (1, 0, 0, 512, me), (1, 513, 0, 1, mo)):
                        K.op(dve, TS1(zv[:, db, dc:dc + 1], zv[:, sb_, sc_:sc_ + 1], mk, ALU.mult), reads=[zr[j], vecr], writes=[zr[j]])
                wcs = [vecs[:, V_WC + l * 6 + tap * 2 + j:V_WC + l * 6 + tap * 2 + j + 1] for tap in range(3)]
                K.op(dve, TS1(yv[:, 0:nsq, :], zv[:, 0:nsq, 1:L + 1], wcs[1], ALU.mult), reads=[zr[j], vecr], writes=[ycr[j]])
                K.op(dve, STT(yv[:, 0:nsq, :], zv[:, 0:nsq, 0:L], wcs[0], yv[:, 0:nsq, :], ALU.mult, ALU.add), reads=[zr[j], vecr, ycr[j]], writes=[ycr[j]])
                K.op(dve, STT(yv[:, 0:nsq, :], zv[:, 0:nsq, 2:L + 2], wcs[2], yv[:, 0:nsq, :], ALU.mult, ALU.add), reads=[zr[j], vecr, ycr[j]], writes=[ycr[j]])

            slot, sres = ring_get(4096)
            wC = slot.rearrange("p (k n) -> p k n", k=8)
            for tb in range(nbq):
                hs = [hT[:, k, sl(tb)] for k in range(8)]
                hres = [hr[k][tb] for k in range(8)]
                for j in range(2):
                    b = wk.next()
                    K.op(pe, MMG(psb[b], [wC[:, k, j * 128:(j + 1) * 128] for k in range(8)], hs), reads=[sres] + hres, writes=[PS[b]])
                    K.op(act, ACTF(uT[:, j, sl(tb)], psb[b], GELU), reads=[PS[b]], writes=[ur[j][tb]])
                for j in range(2):
                    b = wk.next()
                    K.op(pe, MMG(psb[b], [wC[:, k, 256 + j * 128:256 + (j + 1) * 128] for k in range(8)], hs), reads=[sres] + hres, writes=[PS[b]])
                    K.op(dve, CP(bgT[:, j, sl(tb)], psb[b]), reads=[PS[b]], writes=[bgr[j]])
            ring_load()

            for j in range(2):
                K.op(pool, TT(ocT[:, j, 0:nbq * TBS], ycv[:, j, 0:nbq * TBS], bgT[:, j, 0:nbq * TBS], ALU.mult), reads=[ycr[j], bgr[j]], writes=[ocr[j]])
            for tci in range(4 * nbq):
                tb = tci // 4
                tok = slice(tci * 128, (tci + 1) * 128)
                for fc in range(2):
                    b = wk.next()
                    K.op(pe, (lambda b, tci, fc: lambda e: e.matmul(psb[b][:, 0:256], vn[:, tci, fc * 128:(fc + 1) * 128], wsT[:, fc * 256:(fc + 1) * 256], start=True, stop=True))(b, tci, fc),
                         reads=[vnr[tci], wsr], writes=[PS[b]])
                    tmp, tr = tmpR.next()
                    for hl in range(2):
                        pp = slice(hl * 64, (hl + 1) * 64)
                        K.op(dve, STT(tmp[pp, 0:128], psb[b][pp, hl * 128:(hl + 1) * 128], vecs[pp, V_GV + l * 2 + fc:V_GV + l * 2 + fc + 1],
                                      bsb[pp, fc, :], ALU.mult, ALU.add), reads=[PS[b], vecr, bsr], writes=[tr])
                    for hl in range(2):
                        pp = slice(hl * 64, (hl + 1) * 64)
                        K.op(pool, TT(obT[pp, fc, tok], tmp[pp, 0:128], uT[pp, fc, tok], ALU.mult), reads=[tr, ur[fc][tb]], writes=[obr[fc][tb]])

            chk('conv')
            slot0, sres0 = ring_get(4096)
            slot1, sres1 = ring_get(4096)
            wO = [slot0.rearrange("p (k n) -> p k n", k=8), slot1.rearrange("p (k n) -> p k n", k=8)]
            parkw = [big, big2]
            parkwres = [[[bigr[m]] for m in range(8)], [[hr[m][0], hr[m][1]] for m in range(8)]]
            for tb in range(nbq):
                rhs = [qTn[:, h, sl(tb)] for h in range(4)] + [obT[:, j, sl(tb)] for j in range(2)] + [ocT[:, j, sl(tb)] for j in range(2)]
                rres = [oar[h][2 * tb] for h in range(4)] + [oar[h][2 * tb + 1] for h in range(4)] + [obr[j][tb] for j in range(2)] + ocr
                pend = []
                for m in range(8):
                    b = wk.next()
                    w = wO[m // 4]
                    K.op(pe, MMG(psb[b], [w[:, k, (m % 4) * 128:(m % 4 + 1) * 128] for k in range(8)], rhs),
                         reads=[sres0, sres1] + rres, writes=[PS[b]])
                    flush_pend(pend, 4 + tb)
                    K.op(act, ACTF(parkw[tb][:, m, :], psb[b], AF.Identity, scale=DER[:, 1, m:m + 1]), reads=[PS[b], derr[0]], writes=parkwres[tb][m])
                    sq, sqr = sqR.next()
                    K.op(act, ACTF(sq, psb[b], AF.Square), reads=[PS[b]], writes=[sqr])
                    pend.append((sq, sqr, m == 0, m == 7))
                flush_pend(pend, 4 + tb)
                rstd, rr = emit_rstd_bank(4 + tb, D)
                emit_residual(parkw[tb], parkwres[tb], 1, rstd, rr, tb)
            ring_load()
            ring_load()
            chk('wout')

            wk.items = wk_sets['f'] + [6] + ([7] if is_s else [])
            K.phase_switch()
            f1r = [[K.mkA("f1_%d_%d" % (m, tb)) for tb in range(2)] for m in range(32)]
            do_mod_next = (not is_s) and (l + 1 < nlayers)
            emit_prenorm(l, c, 2, 24, 0)
            for j in range(8):
                slot, sres = ring_get(4096)
                w1 = slot.rearrange("p (k n) -> p k n", k=8)
                for tb in range(nbq):
                    if j == 0 and tb == 1:
                        emit_prenorm(l, c, 2, 24, 1)
                    hs = [hT[:, k, sl(tb)] for k in range(8)]
                    hres = [hr[k][tb] for k in range(8)]
                    for mi in range(4):
                        m = 4 * j + mi
                        b = wk.next()
                        K.op(pe, MMG(psb[b], [w1[:, k, mi * 128:(mi + 1) * 128] for k in range(8)], hs), reads=[sres] + hres, writes=[PS[b]])
                        tmp, tr = tmpR.next()
                        K.op(act, ACTF(tmp, psb[b], AF.Relu), reads=[PS[b]], writes=[tr])
                        K.op(dve, TT(f1T[:, m, sl(tb)], tmp, tmp, ALU.mult), reads=[tr], writes=[f1r[m][tb]])
                ring_load()
                if do_mod_next:
                    emit_mod_piece(l + 1, j)
                    if j == 5:
                        emit_mod_evac(l + 1, 0)
            chk('ff1')
            park = [big, big2]
            parkres = [[[bigr[m]] for m in range(8)], [[hr[m][0], hr[m][1]] for m in range(8)]]
            pend = [[], []]

            def ff2_group(m, tb, w2, sres):
                b = wk.next()
                K.op(pe, MMG(psb[b], [w2[:, k, :] for k in range(32)], [f1T[:, k, sl(tb)] for k in range(32)]),
                     reads=[sres] + [f1r[k][tb] for k in range(32)], writes=[PS[b]])
                flush_pend(pend[tb], 4 + tb)
                K.op(act, ACTF(park[tb][:, m, :], psb[b], AF.Identity, scale=DER[:, 3, m:m + 1]), reads=[PS[b], derr[1]], writes=parkres[tb][m])
                sq, sqr = sqR.next()
                K.op(act, ACTF(sq, psb[b], AF.Square), reads=[PS[b]], writes=[sqr])
                pend[tb].append((sq, sqr, m == 0, m == 7))

            for m in range(6):
                slot, sres = ring_get(4096)
                w2 = slot.rearrange("p (k n) -> p k n", k=32)
                for tb in range(nbq):
                    ff2_group(m, tb, w2, sres)
                ring_load()
                if do_mod_next and m < 4:
                    emit_mod_piece(l + 1, 8 + m)
                    if m == 3:
                        emit_mod_evac(l + 1, 1)
            slot6, sres6 = ring_get(4096)
            slot7, sres7 = ring_get(4096)
            w26 = slot6.rearrange("p (k n) -> p k n", k=32)
            w27 = slot7.rearrange("p (k n) -> p k n", k=32)
            for tb in range(nbq):
                ff2_group(6, tb, w26, sres6)
                ff2_group(7, tb, w27, sres7)
                flush_pend(pend[tb], 4 + tb)
                rstd, rr = emit_rstd_bank(4 + tb, D)
                emit_residual(park[tb], parkres[tb], 3, rstd, rr, tb)
            ring_load()
            ring_load()
            K.phase_switch()
            chk('layer%d' % l)

        ncols = TBS if is_s else T
        for k in range(8):
            K.dma(sp, out=y_d[k * 128:(k + 1) * 128, :], in_=xT[:, k, 0:ncols], reads=[xr[k][0], xr[k][1]])

    try:
        emit_pass(False, xp_d, yp_d)
        if do_s:
            emit_pass(True, xs_d, ys_d)
    except _Stop:
        pass

    allres = [r for row in xr for r in row] + stR
    done = {}
    for r in allres:
        if r.dsem is not None:
            done[r.dsem] = max(done.get(r.dsem, 0), r.dcnt)
    for semidx, val in done.items():
        sp.eng.wait_ge(K.sems[semidx], val)
    return nc


def _fm(v):
    return np.ascontiguousarray(v.reshape(-1, 128).T)


def _piece_kn(w, c0, ncols):
    Kd = w.shape[0]
    blk = w[:, c0:c0 + ncols].reshape(Kd // 128, 128, ncols).transpose(1, 0, 2).reshape(128, -1)
    out = np.zeros((128, 4096), np.float32)
    out[:, :blk.shape[1]] = blk
    return out


def _rope_tables():
    rows = T // 64
    row = np.repeat(np.arange(rows, dtype=np.float32), 64)
    col = np.tile(np.arange(64, dtype=np.float32), rows)
    nf = 16
    inv = (np.float32(10000.0) ** (-np.arange(nf, dtype=np.float32) / np.float32(nf))).astype(np.float32)
    ang_r = (row[:, None] * inv).astype(np.float32)
    ang_c = (col[:, None] * inv).astype(np.float32)
    cr, sr, cc, sc = np.cos(ang_r), np.sin(ang_r), np.cos(ang_c), np.sin(ang_c)
    C = np.concatenate([cr, cr, cc, cc], axis=1)
    S = np.concatenate([-sr, sr, -sc, sc], axis=1)
    return np.ascontiguousarray(C.T.astype(np.float32)), np.ascontiguousarray(S.T.astype(np.float32))


_PROG = {}


def kernel(x_prompt, x_sample, cache_ckv, cache_krope, c, c_ctx,
           w_ada, b_ada, g_pre_mix, w_in, g_q, w_uq, g_kv, w_ukv,
           g_v, w_s, b_s, w_conv, w_out, g_post_mix,
           g_pre_ffn, w_ff1, w_ff2, g_post_ffn):
    f = lambda a: np.asarray(a, dtype=np.float32)
    x_prompt, x_sample, cache_ckv, cache_krope, c, c_ctx = map(f, (x_prompt, x_sample, cache_ckv, cache_krope, c, c_ctx))
    w_ada, b_ada, g_pre_mix, w_in, g_q, w_uq, g_kv, w_ukv = map(f, (w_ada, b_ada, g_pre_mix, w_in, g_q, w_uq, g_kv, w_ukv))
    g_v, w_s, b_s, w_conv, w_out, g_post_mix, g_pre_ffn, w_ff1, w_ff2, g_post_ffn = map(
        f, (g_v, w_s, b_s, w_conv, w_out, g_post_mix, g_pre_ffn, w_ff1, w_ff2, g_post_ffn))
    NC = 8
    wp = np.zeros((DEPTH * PIECES_PER_LAYER, 128, 4096), np.float32)
    for l in range(DEPTH):
        b = l * PIECES_PER_LAYER
        for j in range(12):
            wp[b + j] = _piece_kn(w_ada[l], j * 512, 512)
        wi = w_in[l]
        kr = wi[:, 640:704]
        A = np.concatenate([wi[:, 0:384], kr, kr[:, ROPE_PERM]], axis=1)
        B = np.concatenate([wi[:, 384:640], wi[:, 960:1216]], axis=1)
        Cc = np.concatenate([wi[:, 704:960], wi[:, 1216:1472]], axis=1)
        Dd = np.concatenate([wi[:, 1472:1728], wi[:, 1728:1984]], axis=1)
        wp[b + 12] = _piece_kn(A, 0, 512)
        wp[b + 13] = _piece_kn(B, 0, 512)
        wp[b + 14] = _piece_kn(w_ukv[l].reshape(256, 1024), 0, 1024)
        uq = w_uq[l]
        uqx = np.concatenate([uq[:, :, 0:128], uq[:, :, 128:192], uq[:, :, 128:192][:, :, ROPE_PERM]], axis=2).reshape(384, 1024)
        wp[b + 15] = _piece_kn(uqx, 0, 1024)
        wp[b + 16] = _piece_kn(Cc, 0, 512)
        wp[b + 17] = _piece_kn(Dd, 0, 512)
        wp[b + 18] = _piece_kn(w_out[l], 0, 512)
        wp[b + 19] = _piece_kn(w_out[l], 512, 512)
        for j in range(8):
            wp[b + 20 + j] = _piece_kn(w_ff1[l], j * 512, 512)
        for m in range(8):
            wp[b + 28 + m] = _piece_kn(w_ff2[l], m * 128, 128)
    vecs0 = np.zeros((128, NV), np.float32)
    for l in range(DEPTH):
        vecs0[:, V_BADA + l * 48:V_BADA + (l + 1) * 48] = _fm(b_ada[l])
        vecs0[:, V_GPM + l * 8:V_GPM + (l + 1) * 8] = _fm(g_pre_mix[l])
        vecs0[:, V_GPOM + l * 8:V_GPOM + (l + 1) * 8] = _fm(g_post_mix[l])
        vecs0[:, V_GPF + l * 8:V_GPF + (l + 1) * 8] = _fm(g_pre_ffn[l])
        vecs0[:, V_GPOF + l * 8:V_GPOF + (l + 1) * 8] = _fm(g_post_ffn[l])
        vecs0[:, V_GQ + l * 3:V_GQ + (l + 1) * 3] = _fm(g_q[l])
        vecs0[:, V_GKV + l * 2:V_GKV + (l + 1) * 2] = _fm(g_kv[l])
        vecs0[:, V_GV + l * 2:V_GV + (l + 1) * 2] = _fm(g_v[l])
        for tap in range(3):
            vecs0[:, V_WC + l * 6 + tap * 2:V_WC + l * 6 + tap * 2 + 2] = _fm(w_conv[l, tap])
    vecs0[:, V_COND:V_COND + 8] = _fm(c_ctx)
    wst = np.ascontiguousarray(w_s.transpose(0, 3, 1, 2).reshape(DEPTH, 128, 512))
    bsbh = np.zeros((DEPTH, 128, 2, 128), np.float32)
    for fc in range(2):
        for hl in range(2):
            bsbh[:, hl * 64:(hl + 1) * 64, fc, :] = b_s[:, fc * 2 + hl, None, :]
    bsbh = bsbh.reshape(DEPTH, 128, 256)
    ct, st = _rope_tables()

    in_maps = []
    for core in range(NC):
        bidx = core // 2
        v = vecs0.copy()
        v[:, V_COND + 8:V_COND + 16] = _fm(c[bidx])
        half = core % 2
        v[:, V_MASK] = 1.0 if half == 0 else 0.0
        v[:, V_MASK + 1] = 1.0 if half == 1 else 0.0
        perm = np.arange(T) if half == 0 else np.concatenate([np.arange(TBS, T), np.arange(0, TBS)])
        in_maps.append({
            "xp": np.ascontiguousarray(x_prompt[4 * core:4 * core + 4].reshape(T, D).T),
            "xs": np.ascontiguousarray(x_sample[bidx].T[:, perm]),
            "cckv": np.ascontiguousarray(cache_ckv[bidx].transpose(0, 2, 1)),
            "ckr": np.ascontiguousarray(cache_krope[bidx].transpose(0, 2, 1)),
            "vecs": v, "wp": wp, "ct": np.ascontiguousarray(ct[:, perm]), "st": np.ascontiguousarray(st[:, perm]), "wst": wst, "bsb": bsbh,
        })
    if "nc" not in _PROG:
        _PROG["nc"] = build_program()
    res = run_bass_kernel_spmd(_PROG["nc"], in_maps, core_ids=list(range(NC)))
    R = res.results
    y_prompt = np.zeros((32, 256, D), np.float32)
    y_sample = np.zeros((4, 1024, D), np.float32)
    new_ckv = np.zeros((32, DEPTH, 256, 256), np.float32)
    new_krope = np.zeros((32, DEPTH, 256, 64), np.float32)
    for core in range(NC):
        r = R[core]
        y_prompt[4 * core:4 * core + 4] = np.asarray(r["yp"]).T.reshape(4, 256, D)
        hf = core % 2
        y_sample[core // 2, hf * TBS:(hf + 1) * TBS] = np.asarray(r["ys"]).T
        ok = np.asarray(r["ockv"])
        new_ckv[4 * core:4 * core + 4] = ok.reshape(DEPTH, 256, 4, 256).transpose(2, 0, 3, 1)
        okr = np.asarray(r["okr"])
        new_krope[4 * core:4 * core + 4] = okr.reshape(DEPTH, 64, 4, 256).transpose(2, 0, 3, 1)
    return (y_prompt, y_sample, new_ckv, new_krope)
```

```python
import numpy as np
import concourse.bass as bass
import concourse.mybir as mybir
from concourse.bass_utils import run_bass_kernel_spmd

F32 = mybir.dt.float32
BF16 = mybir.dt.bfloat16
AF = mybir.ActivationFunctionType
ALU = mybir.AluOpType
AX = mybir.AxisListType

D = 1024
T = 1024
TBS = 512
DEPTH = 4
EPS = 1e-6
NSLOT = 4
SCALE = 1.0 / float(np.sqrt(192.0))
GELU = AF.Gelu_apprx_tanh
SAME_ENGINE_SYNC = True
ATTACH_WAITS = True
WARM_A = 48
WARM_B = 24

V_BADA, V_GPM, V_GPOM, V_GPF, V_GPOF, V_GQ, V_GKV, V_GV, V_WC, V_COND, V_MASK, NV = (
    0, 192, 224, 256, 288, 320, 332, 340, 348, 372, 388, 390)
PIECES_PER_LAYER = 36
ROPE_PERM = np.concatenate([np.arange(16, 32), np.arange(0, 16), np.arange(48, 64), np.arange(32, 48)])


class Res:
    __slots__ = ("name", "w", "r", "dsem", "dcnt", "excl", "regA")

    def __init__(self, name, excl=False, regA=False):
        self.name = name
        self.excl = excl
        self.regA = regA
        self.w = None
        self.r = {}
        self.dsem = None
        self.dcnt = 0


class Eng:
    def __init__(self, name, eng, semidx, is_pe=False):
        self.name = name
        self.eng = eng
        self.semidx = semidx
        self.is_pe = is_pe
        self.cnt = 0
        self.seen = {}


class KB:
    def __init__(self, nc):
        self.nc = nc
        self.sems = []
        self.dsems = {}
        self.pe = Eng("pe", nc.tensor, self.newsem("pe"), True)
        self.act = Eng("act", nc.scalar, self.newsem("act"))
        self.dve = Eng("dve", nc.vector, self.newsem("dve"))
        self.pool = Eng("pool", nc.gpsimd, self.newsem("pool"))
        self.sp = Eng("sp", nc.sync, None)
        self.compute = [self.pe, self.act, self.dve, self.pool]
        self.nops = 0
        self.region = {}
        self.inherit = {}

    def phase_switch(self):
        self.inherit = dict(self.region)

    def mkA(self, name):
        r = Res(name, regA=True)
        r.r = dict(self.inherit)
        return r

    def _note_region(self, ev, reads, writes):
        for x in list(reads) + list(writes):
            if x.regA:
                semidx, val, src = ev
                cur = self.region.get(semidx)
                if cur is None or cur[0] < val:
                    self.region[semidx] = (val, src)
                return

    def newsem(self, name=None):
        h = self.nc.alloc_semaphore(name or ("s%d" % len(self.sems)))
        self.sems.append(h)
        return len(self.sems) - 1

    def _waits(self, E, reads, writes, attach_last=False):
        need = {}

        def add(ev):
            semidx, val, src = ev
            if src is E and (E.is_pe or not SAME_ENGINE_SYNC):
                return
            if need.get(semidx, 0) < val:
                need[semidx] = val

        for r in reads:
            if r.w is not None:
                add(r.w)
        for w in writes:
            if w.w is not None:
                add(w.w)
            for semidx, (val, src) in w.r.items():
                add((semidx, val, src))
        todo = []
        for semidx, val in need.items():
            if E.seen.get(semidx, 0) >= val:
                continue
            todo.append((semidx, val))
            E.seen[semidx] = val
        attach = None
        if attach_last and todo:
            attach = todo.pop()
        for semidx, val in todo:
            E.eng.wait_ge(self.sems[semidx], val)
        return attach

    def _record(self, ev, reads, writes):
        semidx, val, src = ev
        for r in reads:
            cur = r.r.get(semidx)
            if cur is None or cur[0] < val:
                r.r[semidx] = (val, src)
        for w in writes:
            w.w = ev
            w.r = {}

    def op(self, E, fn, reads=(), writes=()):
        ex = [r for r in reads if r.excl]
        if ex:
            writes = list(writes) + ex
            reads = [r for r in reads if not r.excl]
        attach = self._waits(E, reads, writes, attach_last=(ATTACH_WAITS and E.is_pe))
        if attach is None:
            ins = fn(E.eng)
        elif E.is_pe:
            proxy = _PEProxy(E.eng, self.sems[attach[0]], attach[1])
            ins = fn(proxy)
            assert proxy.att is None
        else:
            ins = fn(E.eng)
            ins._wait_ge(self.sems[attach[0]], attach[1])
        E.cnt += 1
        ins.then_inc(self.sems[E.semidx], 1)
        self._record((E.semidx, E.cnt, E), reads, writes)
        self._note_region((E.semidx, E.cnt, E), reads, writes)
        self.nops += 1

    def dma(self, Q, out, in_, reads=(), writes=()):
        self._waits(Q, reads, writes)
        res = writes[0] if writes else reads[0]
        ent = self.dsems.get(res.name)
        if ent is None:
            ent = self.dsems[res.name] = [self.newsem("d_" + res.name), 0]
        ent[1] += 16
        res.dsem, res.dcnt = ent[0], ent[1]
        Q.eng.dma_start(out=out, in_=in_).then_inc(self.sems[res.dsem], 16)
        self._record((res.dsem, res.dcnt, None), reads, writes)
        self._note_region((res.dsem, res.dcnt, None), reads, writes)

    def barrier(self, resources=()):
        engs = self.compute + [self.sp]
        snap = {F: F.cnt for F in self.compute}
        dm = {}
        for r in resources:
            if r.dsem is not None and r.dcnt > 0:
                dm[r.dsem] = max(dm.get(r.dsem, 0), r.dcnt)
        for E in engs:
            for F in self.compute:
                if F is E or snap[F] == 0:
                    continue
                if E.seen.get(F.semidx, 0) < snap[F]:
                    E.eng.wait_ge(self.sems[F.semidx], snap[F])
                    E.seen[F.semidx] = snap[F]
            for semidx, val in dm.items():
                if E.seen.get(semidx, 0) < val:
                    E.eng.wait_ge(self.sems[semidx], val)
                    E.seen[semidx] = val


class _PEProxy:
    def __init__(self, eng, sem, val):
        self.eng = eng
        self.att = (sem, val)

    def matmul(self, *a, **k):
        ins = self.eng.matmul(*a, **k)
        if self.att is not None:
            ins._wait_ge(self.att[0], self.att[1])
            self.att = None
        return ins


class Rot:
    def __init__(self, items):
        self.items = items
        self.i = 0

    def next(self):
        it = self.items[self.i % len(self.items)]
        self.i += 1
        return it


def ACTF(out, in_, func, bias=None, scale=None):
    kw = {}
    if bias is not None:
        kw["bias"] = bias
    if scale is not None:
        kw["scale"] = scale
    return lambda e: e.activation(out=out, in_=in_, func=func, **kw)


def TT(out, in0, in1, op):
    return lambda e: e.tensor_tensor(out=out, in0=in0, in1=in1, op=op)


def STT(out, in0, scalar, in1, op0, op1):
    return lambda e: e.scalar_tensor_tensor(out=out, in0=in0, scalar=scalar, in1=in1, op0=op0, op1=op1)


def TS1(out, in0, s1, op0):
    return lambda e: e.tensor_scalar(out=out, in0=in0, scalar1=s1, scalar2=None, op0=op0)


def CP(out, in_):
    return lambda e: e.tensor_copy(out=out, in_=in_)


def RCP(out, in_):
    return lambda e: e.reciprocal(out=out, in_=in_)


def MMG(out_ps, lhs_list, rhs_list):
    def fn(pe):
        n = len(lhs_list)
        ins = None
        for i in range(n):
            ins = pe.matmul(out_ps, lhs_list[i], rhs_list[i], start=(i == 0), stop=(i == n - 1))
        return ins
    return fn


class _Stop(Exception):
    pass


def build_program(nlayers=DEPTH, do_s=True, stop=None):
    nc = bass.Bass("TRN2", target_bir_lowering=False)

    def chk(label):
        if stop is not None and label == stop:
            raise _Stop()
    dt_in = lambda name, shape: nc.dram_tensor(name, shape, F32, kind="ExternalInput").ap()
    dt_out = lambda name, shape: nc.dram_tensor(name, shape, F32, kind="ExternalOutput").ap()
    xp_d = dt_in("xp", [D, T])
    xs_d = dt_in("xs", [D, T])
    cckv_d = dt_in("cckv", [DEPTH, 256, 512])
    ckr_d = dt_in("ckr", [DEPTH, 64, 512])
    vecs_d = dt_in("vecs", [128, NV])
    wp_d = dt_in("wp", [DEPTH * PIECES_PER_LAYER, 128, 4096])
    ct_d = dt_in("ct", [64, T])
    st_d = dt_in("st", [64, T])
    wst_d = dt_in("wst", [DEPTH, 128, 512])
    bsb_d = dt_in("bsb", [DEPTH, 128, 256])
    yp_d = dt_out("yp", [D, T])
    ys_d = dt_out("ys", [D, TBS])
    ockv_d = dt_out("ockv", [DEPTH, 256, T])
    okr_d = dt_out("okr", [DEPTH, 64, T])

    off = [0]

    def alloc(nwords):
        o = off[0]
        off[0] += (nwords + 7) // 8 * 8
        return o

    O_X = alloc(8192)
    O_H = alloc(4096)
    O_RING = alloc(2048 * NSLOT)
    O_BIG = alloc(4096)
    O_VECS = alloc(NV)
    O_MOD = alloc(DEPTH * 2 * 48)
    O_DER = alloc(32)
    O_SC = alloc(8)
    O_ONES = alloc(64)
    O_EPS = alloc(8)
    O_RSTD = alloc(1024)
    O_SQ = alloc(768)
    O_TMP = alloc(1536)
    O_PT = alloc(768)
    O_RDEN = alloc(1024)
    O_ST = alloc(3072)
    O_WST = alloc(256)
    O_BSB = alloc(256)
    O_VSQ = alloc(256)
    O_VSS = alloc(8)
    O_CGS = alloc(1024)
    O_GV2 = alloc(1024)
    O_A = alloc(16384)
    NW = off[0]
    assert NW * 4 <= 212000, NW * 4

    arena = nc.alloc_sbuf_tensor("arena", [128, NW], F32)
    ps = nc.alloc_psum_tensor("ps", [128, 8, 512], F32)

    def fv(o, n):
        return arena[:, o:o + n]

    def bv(o, nwords):
        return arena[:, o:o + nwords].bitcast(BF16)

    xT = fv(O_X, 8192).rearrange("p (k t) -> p k t", k=8)
    hT = bv(O_H, 4096).rearrange("p (k t) -> p k t", k=8)
    big2 = fv(O_H, 4096).rearrange("p (k t) -> p k t", k=8)
    slots = [bv(O_RING + 2048 * i, 2048) for i in range(NSLOT)]
    big = fv(O_BIG, 4096).rearrange("p (k t) -> p k t", k=8)
    vecs = fv(O_VECS, NV)
    MOD = fv(O_MOD, DEPTH * 2 * 48).rearrange("p (l c j) -> p l c j", l=DEPTH, c=2)
    DER = fv(O_DER, 32).rearrange("p (a k) -> p a k", a=4)
    scT = bv(O_SC, 8).rearrange("p (k c) -> p k c", k=8)
    ones = bv(O_ONES, 64)
    epsT = fv(O_EPS, 8)
    rstds = [fv(O_RSTD + 512 * i, 512) for i in range(2)]
    sqs = [bv(O_SQ + 256 * i, 256) for i in range(3)]
    tmps = [fv(O_TMP + 512 * i, 512) for i in range(3)]
    pts = [bv(O_PT + 256 * i, 256) for i in range(3)]
    rdens = [fv(O_RDEN + 512 * i, 512) for i in range(2)]
    ckv_st = [fv(O_ST + 1024 * i, 1024).rearrange("p (j t) -> p j t", j=2) for i in range(2)]
    kr_st = [fv(O_ST + 2048 + 512 * i, 512) for i in range(2)]
    CT = fv(O_ST, 1024)
    ST = fv(O_ST + 1024, 1024)
    wsT = bv(O_WST, 256)
    bsb = fv(O_BSB, 256).rearrange("p (f q) -> p f q", f=2)
    gvs = [fv(O_CGS + 256 * i, 256) for i in range(4)] + [fv(O_GV2 + 256 * i, 256) for i in range(4)]
    vsq = fv(O_VSQ, 256)
    vss = fv(O_VSS, 8)
    cgss = [fv(O_CGS + 512 * i, 512) for i in range(2)]
    qTn = bv(O_A + 0, 2048).rearrange("p (h t) -> p h t", h=4)
    qTr = bv(O_A + 2048, 2048).rearrange("p (h t) -> p h t", h=4)
    knT = bv(O_A + 4096, 3072).rearrange("p (h t) -> p h t", h=4)
    va = bv(O_A + 7168, 3072).rearrange("p (c n) -> p c n", c=12)
    krTb = bv(O_A + 10240, 768)
    vn = bv(O_A + 11008, 1024).rearrange("p (c n) -> p c n", c=8)
    qnT = bv(O_A + 12032, 1536).rearrange("p (k t) -> p k t", k=3)
    ckvTb = bv(O_A + 13568, 1536).rearrange("p (j t) -> p j t", j=2)
    uT = bv(O_A + 2048, 1024).rearrange("p (j t) -> p j t", j=2)
    obT = bv(O_A + 3072, 1024).rearrange("p (j t) -> p j t", j=2)
    bgT = bv(O_A + 4096, 1024).rearrange("p (j t) -> p j t", j=2)
    ocT = bv(O_A + 5120, 1024).rearrange("p (j t) -> p j t", j=2)
    zpad = fv(O_A + 6144, 2064).rearrange("p (j t) -> p j t", j=2)
    ycv = fv(O_A + 8208, 2048).rearrange("p (j t) -> p j t", j=2)
    f1T = bv(O_A, 16384).rearrange("p (m t) -> p m t", m=32)

    K = KB(nc)
    pe, act, dve, pool, sp = K.pe, K.act, K.dve, K.pool, K.sp

    PS = [Res("ps%d" % i, excl=True) for i in range(8)]
    psb = [ps[:, i, :] for i in range(8)]
    wk = Rot([0, 1, 2, 3])
    wk_sets = {'m1': [0, 1, 2, 3, 4, 5, 7], 'm2': [0, 1, 2, 3], 'm3': [0, 1, 2, 3, 7], 'f': [0, 1, 2, 3]}
    xr = [[Res("x%d_%d" % (k, tb)) for tb in range(2)] for k in range(8)]
    hr = [[Res("h%d_%d" % (k, tb)) for tb in range(2)] for k in range(8)]
    slotr = [Res("slot%d" % i) for i in range(NSLOT)]
    bigr = [Res("big%d" % i) for i in range(8)]
    vecr = Res("vecs")
    modr = [[Res("mod%d_%d" % (l, p)) for p in range(2)] for l in range(DEPTH)]
    derr = [Res("der0"), Res("der1")]
    scr = Res("sc")
    constr = Res("const")
    rstdR = Rot([(rstds[i], Res("rstd%d" % i)) for i in range(2)])
    sqR = Rot([(sqs[i], Res("sq%d" % i)) for i in range(3)])
    tmpR = Rot([(tmps[i], Res("tmp%d" % i)) for i in range(3)])
    ptR = Rot([(pts[i], Res("pt%d" % i)) for i in range(3)])
    rdenR = Rot([(rdens[i], Res("rden%d" % i)) for i in range(2)])
    stckv = [[Res("st_ckv%d_%d" % (tb, j)) for j in range(2)] for tb in range(2)]
    stkr = [Res("st_kr0"), Res("st_kr1")]
    stR = [stckv[0][0], stckv[0][1], stckv[1][0], stckv[1][1], stkr[0], stkr[1]]
    ropeR = Res("rope")
    wsr = Res("wst")
    bsr = Res("bsb")
    halfr = [Res("half%d" % i) for i in range(8)]
    vsqr = Res("vsq")
    vssr = [Res("vss0"), Res("vss1")]
    cgsR = Rot([(cgss[i], (halfr[2 * i], halfr[2 * i + 1])) for i in range(2)])

    def sl(tb):
        return slice(tb * TBS, (tb + 1) * TBS)

    K.op(pool, lambda e: e.memset(ones, 1.0), writes=[constr])
    K.op(pool, lambda e: e.memset(epsT, EPS), writes=[constr])
    K.dma(sp, out=vecs, in_=vecs_d, writes=[vecr])

    seq = []
    seq += [(0 * PIECES_PER_LAYER + j, 4096) for j in range(6)]

    def stage_seq(l, is_s):
        b = l * PIECES_PER_LAYER
        first = [(b + 12, 4096), (b + 13, 4096), (b + 15, 3072), (b + 14, 2048)]
        mid = []
        if (not is_s) and l == 0:
            mid += [(j, 4096) for j in range(6, 12)]
        rest = [(b + 17, 4096), (b + 16, 4096), (b + 18, 4096), (b + 19, 4096)]
        for i in range(16):
            rest.append((b + 20 + i, 4096))
            if (not is_s) and l + 1 < nlayers and i < 12:
                rest.append(((l + 1) * PIECES_PER_LAYER + i, 4096))
        return first + mid + rest

    for l in range(nlayers):
        seq += stage_seq(l, False)
    if do_s:
        for l in range(nlayers):
            seq += stage_seq(l, True)

    ring = {"nload": 0, "nuse": 0}

    def ring_load():
        i = ring["nload"]
        if i >= len(seq):
            return
        idx, n = seq[i]
        s = i % NSLOT
        K.dma(pool, out=slots[s][:, 0:n], in_=wp_d[idx, :, 0:n], writes=[slotr[s]])
        ring["nload"] += 1

    def ring_get(expect_n):
        i = ring["nuse"]
        ring["nuse"] += 1
        assert i < ring["nload"], "ring underflow"
        assert seq[i][1] == expect_n, (i, seq[i], expect_n)
        return slots[i % NSLOT], slotr[i % NSLOT]

    for _ in range(NSLOT):
        ring_load()

    for c in range(2):
        K.op(act, ACTF(scT[:, :, c], vecs[:, V_COND + c * 8:V_COND + c * 8 + 8], AF.Silu), reads=[vecr], writes=[scr])

    modps = psb[7][:, 0:96].rearrange("p (j c) -> p j c", c=2)

    def emit_mod_piece(l, j):
        slot, sres = ring_get(4096)
        w = slot.rearrange("p (k n) -> p k n", k=8)

        def fn(e):
            ins = None
            for jj in range(4):
                ch = 4 * j + jj
                for k in range(8):
                    ins = e.matmul(modps[:, ch, :], w[:, k, jj * 128:(jj + 1) * 128], scT[:, k, :],
                                   start=(k == 0), stop=(k == 7))
            return ins
        K.op(pe, fn, reads=[sres, scr], writes=[PS[7]])
        ring_load()

    def emit_mod_evac(l, part):
        j0 = 24 * part
        for c in range(2):
            K.op(dve, TT(MOD[:, l, c, j0:j0 + 24], modps[:, j0:j0 + 24, c], vecs[:, V_BADA + l * 48 + j0:V_BADA + l * 48 + j0 + 24], ALU.add),
                 reads=[PS[7], vecr], writes=[modr[l][part]])

    def emit_mod(l, part):
        for j in range(6 * part, 6 * part + 6):
            emit_mod_piece(l, j)
        emit_mod_evac(l, part)

    def emit_derive(l, c, part):
        M = MOD[:, l, c, :]
        specs = [(0, 8, V_GPM, True), (1, 16, V_GPOM, False)] if part == 0 else [(2, 32, V_GPF, True), (3, 40, V_GPOF, False)]
        for a, sci, gi, plus1 in specs:
            g = vecs[:, gi + l * 8:gi + l * 8 + 8]
            if plus1:
                K.op(dve, STT(DER[:, a, :], M[:, sci:sci + 8], 1.0, g, ALU.add, ALU.mult), reads=[modr[l][part], vecr], writes=[derr[part]])
            else:
                K.op(dve, TT(DER[:, a, :], M[:, sci:sci + 8], g, ALU.mult), reads=[modr[l][part], vecr], writes=[derr[part]])

    def emit_rstd(n_feat):
        return emit_rstd_bank(6, n_feat)

    def emit_rstd_bank(bank, n_feat):
        rstd, rr = rstdR.next()
        K.op(act, ACTF(rstd, psb[bank], AF.Ln, bias=epsT[:, 0:1], scale=1.0 / n_feat), reads=[PS[bank], constr], writes=[rr])
        K.op(act, ACTF(rstd, rstd, AF.Exp, scale=-0.5), reads=[rr], writes=[rr])
        return rstd, rr

    def emit_warm(n):
        if n <= 0:
            return
        b = wk.next()

        def fn(e):
            ins = None
            for _ in range(n):
                ins = e.matmul(psb[b], ones, sqs[0], start=True, stop=True)
            return ins
        K.op(pe, fn, reads=[constr], writes=[PS[b]])

    def emit_prenorm(l, c, a_idx, b_off, tb):
        part = 0 if a_idx == 0 else 1
        for k in range(8):
            sq, sqr = sqR.next()
            if True:
                K.op(act, ACTF(sq, xT[:, k, sl(tb)], AF.Square), reads=[xr[k][tb]], writes=[sqr])
            else:
                K.op(pool, TT(sq, xT[:, k, sl(tb)], xT[:, k, sl(tb)], ALU.mult), reads=[xr[k][tb]], writes=[sqr])
            K.op(pe, (lambda sq, k: lambda e: e.matmul(psb[6], ones, sq, start=(k == 0), stop=(k == 7)))(sq, k),
                 reads=[sqr, constr], writes=[PS[6]])
        rstd, rr = emit_rstd(D)
        for k in range(8):
            tmp, tr = tmpR.next()
            K.op(dve, TT(tmp, xT[:, k, sl(tb)], rstd, ALU.mult),
                 reads=[xr[k][tb], rr], writes=[tr])
            K.op(act, ACTF(hT[:, k, sl(tb)], tmp, AF.Identity, bias=MOD[:, l, c, b_off + k:b_off + k + 1], scale=DER[:, a_idx, k:k + 1]),
                 reads=[tr, modr[l][part], derr[part]], writes=[hr[k][tb]])

    def flush_pend(pend, ssum_bank, keep=0):
        while len(pend) > keep:
            sq, sqr, first, last = pend.pop(0)
            K.op(pe, (lambda sq, first, last: lambda e: e.matmul(psb[ssum_bank], ones, sq, start=first, stop=last))(sq, first, last),
                 reads=[sqr, constr], writes=[PS[ssum_bank]])

    def emit_residual(src, sres_list, c_idx, rstd, rr, tb):
        for m in range(8):
            E = dve
            K.op(E, TT(src[:, m, :], src[:, m, :], rstd, ALU.mult),
                 reads=list(sres_list[m]) + [rr], writes=list(sres_list[m]))
            K.op(E, TT(xT[:, m, sl(tb)], xT[:, m, sl(tb)], src[:, m, :], ALU.add),
                 reads=list(sres_list[m]) + [xr[m][tb]], writes=[xr[m][tb]])

    def emit_pass(is_s, x_d, y_d):
        c = 1 if is_s else 0
        NSEQ, L = (2, 512) if is_s else (4, 256)
        KOFF = 512 if is_s else 0
        NKB = 3 if is_s else 2
        for k in range(8):
            K.dma(sp, out=xT[:, k, :], in_=x_d[k * 128:(k + 1) * 128, :], writes=[xr[k][0], xr[k][1]])
        if is_s:
            K.dma(sp, out=CT[0:64, :], in_=ct_d, writes=[ropeR] + stR)
            K.dma(sp, out=ST[0:64, :], in_=st_d, writes=[ropeR])

        for l in range(nlayers):
            nbq = 1 if (is_s and l == nlayers - 1) else 2
            qnr2 = [[K.mkA("qTn%d_%d" % (h, q)) for q in range(4)] for h in range(4)]
            qrr = [[K.mkA("qTr%d_%d" % (h, tb)) for tb in range(2)] for h in range(4)]
            knr = [[K.mkA("kn%d_%d" % (h, kb)) for kb in range(3)] for h in range(4)]
            var = [K.mkA("va%d" % i) for i in range(12)]
            krbr = [K.mkA("krb%d" % kb) for kb in range(3)]
            vnr = [K.mkA("vn%d" % i) for i in range(8)]
            qnr = [[K.mkA("qn%d_%d" % (j, tb)) for tb in range(2)] for j in range(3)]
            ckvbr = [[K.mkA("ckvb%d_%d" % (j, kb)) for kb in range(3)] for j in range(2)]
            regA = ([r for row in qnr2 for r in row] + [r for row in qrr for r in row] + [r for row in knr for r in row]
                    + var + krbr + vnr + [r for row in qnr for r in row] + [r for row in ckvbr for r in row])

            if (not is_s) and l == 0:
                emit_mod(0, 0)
            emit_derive(l, c, 0)

            K.dma(pool, out=wsT, in_=wst_d[l], writes=[wsr])
            K.dma(sp, out=bsb, in_=bsb_d[l].rearrange("p (f q) -> p f q", f=2), writes=[bsr])
            if is_s:
                for j in range(2):
                    K.dma(pool, out=ckvTb[:, j, 0:512], in_=cckv_d[l, j * 128:(j + 1) * 128, :], writes=[ckvbr[j][0]])
                K.dma(pool, out=krTb[0:64, 0:512], in_=ckr_d[l], writes=[krbr[0]])

            chk('mod')
            wk.items = wk_sets['m1']
            emit_prenorm(l, c, 0, 0, 0)
            emit_warm(WARM_A)
            chk('prenorm')

            slot, sres = ring_get(4096)
            wA = slot.rearrange("p (k n) -> p k n", k=8)
            for tb in range(2):
                if tb == 1:
                    emit_prenorm(l, c, 0, 0, 1)
                    emit_warm(WARM_B)
                hs = [hT[:, k, sl(tb)] for k in range(8)]
                hres = [hr[k][tb] for k in range(8)]
                pendA = []
                for j in range(3 if tb < nbq else 0):
                    b = wk.next()
                    K.op(pe, MMG(psb[b], [wA[:, k, j * 128:(j + 1) * 128] for k in range(8)], hs),
                         reads=[sres] + hres, writes=[PS[b]])
                    flush_pend(pendA, 6)
                    K.op(dve, CP(big[:, j, :], psb[b]), reads=[PS[b]], writes=[bigr[j]])
                    sq, sqr = sqR.next()
                    K.op(act, ACTF(sq, big[:, j, :], AF.Square), reads=[bigr[j]], writes=[sqr])
                    pendA.append((sq, sqr, j == 0, j == 2))
                b = wk.next()
                K.op(pe, MMG(psb[b][0:64, :], [wA[:, k, 384:448] for k in range(8)], hs), reads=[sres] + hres, writes=[PS[b]])
                flush_pend(pendA, 6)
                if tb < nbq:
                    rstd, rr = emit_rstd(384)
                for j in range(3 if tb < nbq else 0):
                    K.op(dve, STT(qnT[:, j, sl(tb)], big[:, j, :], vecs[:, V_GQ + l * 3 + j:V_GQ + l * 3 + j + 1], rstd,
                                  ALU.mult, ALU.mult), reads=[bigr[j], rr, vecr], writes=[qnr[j][tb]])
                kdst = krTb[0:64, KOFF + tb * TBS:KOFF + (tb + 1) * TBS]
                kres = krbr[(KOFF // 512) + tb]
                if not is_s:
                    K.op(act, ACTF(kr_st[tb][0:64, :], psb[b][0:64, :], AF.Copy), reads=[PS[b]], writes=[stkr[tb]])
                    K.dma(sp, out=okr_d[l, :, sl(tb)], in_=kr_st[tb][0:64, :], reads=[stkr[tb]])
                    K.op(dve, CP(kdst, kr_st[tb][0:64, :]), reads=[stkr[tb]], writes=[kres])
                else:
                    b2 = wk.next()
                    K.op(pe, MMG(psb[b2][0:64, :], [wA[:, k, 448:512] for k in range(8)], hs), reads=[sres] + hres, writes=[PS[b2]])
                    t1, t1r = tmpR.next()
                    t2, t2r = tmpR.next()
                    K.op(dve, TT(t1[0:64, :], psb[b][0:64, :], CT[0:64, sl(tb)], ALU.mult), reads=[PS[b], ropeR], writes=[t1r])
                    K.op(dve, TT(t2[0:64, :], psb[b2][0:64, :], ST[0:64, sl(tb)], ALU.mult), reads=[PS[b2], ropeR], writes=[t2r])
                    K.op(pool, TT(kdst, t1[0:64, :], t2[0:64, :], ALU.add), reads=[t1r, t2r], writes=[kres])
            ring_load()

            chk('A')
            def emit_v_mm(tb, tc, wB, sres, hres):
                tok = slice(tb * TBS + tc * 128, tb * TBS + (tc + 1) * 128)
                b = wk.next()
                K.op(pe, MMG(psb[b][:, 0:256], [hT[:, k, tok] for k in range(8)], [wB[:, k, 256:512] for k in range(8)]),
                     reads=[sres] + hres, writes=[PS[b]])
                K.op(act, ACTF(gvs[tb * 4 + tc], psb[b][:, 0:256], GELU), reads=[PS[b]], writes=[halfr[tb * 4 + tc]])

            def emit_v_norm_a(tb):
                for tc in range(4):
                    gi = tb * 4 + tc
                    K.op(dve, TT(vsq, gvs[gi], gvs[gi], ALU.mult), reads=[halfr[gi]], writes=[vsqr])
                    K.op(dve, (lambda tc: lambda e: e.reduce_sum(out=vss[:, tb * 4 + tc:tb * 4 + tc + 1], in_=vsq, axis=AX.X))(tc), reads=[vsqr], writes=[vssr[tb]])

            def emit_v_norm_b(tb):
                K.op(act, ACTF(vss[:, tb * 4:tb * 4 + 4], vss[:, tb * 4:tb * 4 + 4], AF.Ln, bias=epsT[:, 0:1], scale=1.0 / 256), reads=[vssr[tb], constr], writes=[vssr[tb]])
                K.op(act, ACTF(vss[:, tb * 4:tb * 4 + 4], vss[:, tb * 4:tb * 4 + 4], AF.Exp, scale=-0.5), reads=[vssr[tb]], writes=[vssr[tb]])
                for tc in range(4):
                    gi = tb * 4 + tc
                    K.op(dve, TS1(vn[:, gi, :], gvs[gi], vss[:, gi:gi + 1], ALU.mult), reads=[halfr[gi], vssr[tb]], writes=[vnr[gi]])

            slot, sres = ring_get(4096)
            wB = slot.rearrange("p (k n) -> p k n", k=8)
            for tb in range(2):
                hs = [hT[:, k, sl(tb)] for k in range(8)]
                hres = [hr[k][tb] for k in range(8)]
                pendB = []
                for j in range(2):
                    b = wk.next()
                    K.op(pe, MMG(psb[b], [wB[:, k, j * 128:(j + 1) * 128] for k in range(8)], hs),
                         reads=[sres] + hres, writes=[PS[b]])
                    flush_pend(pendB, 6)
                    K.op(dve, CP(big[:, 4 + j, :], psb[b]), reads=[PS[b]], writes=[bigr[4 + j]])
                    sq, sqr = sqR.next()
                    K.op(act, ACTF(sq, big[:, 4 + j, :], AF.Square), reads=[bigr[4 + j]], writes=[sqr])
                    pendB.append((sq, sqr, j == 0, j == 1))
                kb = (KOFF // 512) + tb
                for tc in range(4 if tb < nbq else 0):
                    emit_v_mm(tb, tc, wB, sres, hres)
                flush_pend(pendB, 6)
                rstd, rr = emit_rstd(256)
                for j in range(2):
                    gsc = vecs[:, V_GKV + l * 2 + j:V_GKV + l * 2 + j + 1]
                    cdst = ckvTb[:, j, KOFF + tb * TBS:KOFF + (tb + 1) * TBS]
                    if not is_s:
                        K.op(dve, STT(ckv_st[tb][:, j, :], big[:, 4 + j, :], gsc, rstd, ALU.mult, ALU.mult),
                             reads=[bigr[4 + j], rr, vecr], writes=[stckv[tb][j]])
                        K.dma(sp, out=ockv_d[l, j * 128:(j + 1) * 128, sl(tb)], in_=ckv_st[tb][:, j, :], reads=[stckv[tb][j]])
                        K.op(act, ACTF(cdst, ckv_st[tb][:, j, :], AF.Copy), reads=[stckv[tb][j]], writes=[ckvbr[j][kb]])
                    else:
                        K.op(dve, STT(cdst, big[:, 4 + j, :], gsc, rstd, ALU.mult, ALU.mult),
                             reads=[bigr[4 + j], rr, vecr], writes=[ckvbr[j][kb]])
            ring_load()

            chk('U')
            K.op(pool, lambda e: e.memset(qTr[64:128, :, :], 0.0), writes=[r for row in qrr for r in row])
            K.op(pool, lambda e: e.memset(krTb[64:128, :], 0.0), writes=krbr)
            slot, sres = ring_get(3072)
            wQ = slot[:, 0:3072].rearrange("p (k n) -> p k n", k=3)
            for tb in range(nbq):
                qs = [qnT[:, kc, sl(tb)] for kc in range(3)]
                qres = [qnr[kc][tb] for kc in range(3)]
                for h in range(4):
                    b = wk.next()
                    K.op(pe, MMG(psb[b], [wQ[:, kc, h * 256:h * 256 + 128] for kc in range(3)], qs),
                         reads=[sres] + qres, writes=[PS[b]])
                    K.op(act, ACTF(qTn[:, h, sl(tb)], psb[b], AF.Copy), reads=[PS[b]], writes=[qnr2[h][2 * tb], qnr2[h][2 * tb + 1]])
                    b = wk.next()
                    K.op(pe, MMG(psb[b][0:64, :], [wQ[:, kc, h * 256 + 128:h * 256 + 192] for kc in range(3)], qs),
                         reads=[sres] + qres, writes=[PS[b]])
                    if not is_s:
                        K.op(dve, CP(qTr[0:64, h, sl(tb)], psb[b][0:64, :]), reads=[PS[b]], writes=[qrr[h][tb]])
                    else:
                        b2 = wk.next()
                        K.op(pe, MMG(psb[b2][0:64, :], [wQ[:, kc, h * 256 + 192:h * 256 + 256] for kc in range(3)], qs),
                             reads=[sres] + qres, writes=[PS[b2]])
                        t1, t1r = tmpR.next()
                        t2, t2r = tmpR.next()
                        K.op(dve, TT(t1[0:64, :], psb[b][0:64, :], CT[0:64, sl(tb)], ALU.mult), reads=[PS[b], ropeR], writes=[t1r])
                        K.op(dve, TT(t2[0:64, :], psb[b2][0:64, :], ST[0:64, sl(tb)], ALU.mult), reads=[PS[b2], ropeR], writes=[t2r])
                        K.op(pool, TT(qTr[0:64, h, sl(tb)], t1[0:64, :], t2[0:64, :], ALU.add), reads=[t1r, t2r], writes=[qrr[h][tb]])
            ring_load()

            chk('B')
            slot, sres = ring_get(2048)
            wU = slot[:, 0:2048].rearrange("p (k n) -> p k n", k=2)
            evac = Rot([act, dve])
            for h in range(4):
                for kb in range(NKB):
                    b = wk.next()
                    K.op(pe, MMG(psb[b], [wU[:, kc, h * 256:h * 256 + 128] for kc in range(2)],
                                 [ckvTb[:, kc, kb * 512:(kb + 1) * 512] for kc in range(2)]),
                         reads=[sres, ckvbr[0][kb], ckvbr[1][kb]], writes=[PS[b]])
                    E = evac.next()
                    dst = knT[:, h, kb * 512:(kb + 1) * 512]
                    if E is act:
                        K.op(act, ACTF(dst, psb[b], AF.Copy), reads=[PS[b]], writes=[knr[h][kb]])
                    else:
                        K.op(dve, CP(dst, psb[b]), reads=[PS[b]], writes=[knr[h][kb]])
            for kch in range(NKB * 4):
                kb = kch // 4
                b = wk.next()

                def fnv(e, b=b, kch=kch):
                    ins = None
                    for h in range(4):
                        for kc in range(2):
                            ins = e.matmul(psb[b][:, h * 128:(h + 1) * 128], ckvTb[:, kc, kch * 128:(kch + 1) * 128],
                                           wU[:, kc, h * 256 + 128:h * 256 + 256], start=(kc == 0), stop=(kc == 1))
                    return ins
                K.op(pe, fnv, reads=[sres, ckvbr[0][kb], ckvbr[1][kb]], writes=[PS[b]])
                E = evac.next()
                if E is act:
                    K.op(act, ACTF(va[:, kch, :], psb[b], AF.Copy), reads=[PS[b]], writes=[var[kch]])
                else:
                    K.op(dve, CP(va[:, kch, :], psb[b]), reads=[PS[b]], writes=[var[kch]])
            ring_load()

            chk('Q')
            wk.items = wk_sets['m2']
            if is_s:
                units = [(qb, h, slice(qb * 512, (qb + 1) * 512), list(range(12)), [2 * qb, 2 * qb + 1], qb) for qb in range(nbq) for h in range(4)]
            else:
                units = [(s, h, slice(s * 256, (s + 1) * 256), [2 * s, 2 * s + 1], [s], s // 2) for s in range(4) for h in range(4)]
            accR = Rot([(4, 6), (5, 7)])
            vnorm_at = {1: (emit_v_norm_a, 0), 3: (emit_v_norm_b, 0)}
            if nbq == 2:
                vnorm_at.update({4: (emit_v_norm_a, 1), 6: (emit_v_norm_b, 1)})
            for ui, (u0, h, qsl, kchs, quarters, tbq) in enumerate(units):
                if ui in vnorm_at:
                    vnorm_at[ui][0](vnorm_at[ui][1])
                nq = qsl.stop - qsl.start
                ob, db = accR.next()
                qres = [qnr2[h][q] for q in quarters] + [qrr[h][tbq]]

                def emit_scores(kch):
                    b = wk.next()
                    kb = kch // 4

                    def fn(e, b=b, kch=kch):
                        e.matmul(psb[b][:, 0:nq], knT[:, h, kch * 128:(kch + 1) * 128], qTn[:, h, qsl], start=True, stop=False)
                        return e.matmul(psb[b][:, 0:nq], krTb[:, kch * 128:(kch + 1) * 128], qTr[:, h, qsl], start=False, stop=True)
                    K.op(pe, fn, reads=[knr[h][kb], krbr[kb]] + qres, writes=[PS[b]])
                    return b
                bq = [emit_scores(kchs[0])]
                if len(kchs) > 1:
                    bq.append(emit_scores(kchs[1]))
                for i, kch in enumerate(kchs):
                    pt, ptr = ptR.next()
                    K.op(act, ACTF(pt[:, 0:nq], psb[bq[i]][:, 0:nq], AF.Exp, scale=SCALE), reads=[PS[bq[i]]], writes=[ptr])
                    if i + 2 < len(kchs):
                        bq.append(emit_scores(kchs[i + 2]))
                    first, last = (i == 0), (i == len(kchs) - 1)

                    def fpv(e, pt=pt, kch=kch, first=first, last=last):
                        e.matmul(psb[ob][:, 0:nq], va[:, kch, h * 128:(h + 1) * 128], pt[:, 0:nq], start=first, stop=last)
                        return e.matmul(psb[db][:, 0:nq], ones, pt[:, 0:nq], start=first, stop=last)
                    K.op(pe, fpv, reads=[ptr, var[kch], constr], writes=[PS[ob], PS[db]])
                rden, rdr = rdenR.next()
                K.op(act, ACTF(rden[:, 0:nq], psb[db][:, 0:nq], AF.Ln), reads=[PS[db]], writes=[rdr])
                K.op(act, ACTF(rden[:, 0:nq], rden[:, 0:nq], AF.Exp, scale=-1.0), reads=[rdr], writes=[rdr])
                K.op(dve, TT(qTn[:, h, qsl], psb[ob][:, 0:nq], rden[:, 0:nq], ALU.mult),
                     reads=[PS[ob], rdr], writes=[qnr2[h][q] for q in quarters])
            oar = qnr2

            chk('attn')
            if (not is_s) and l == 0:
                emit_mod(0, 1)
            emit_derive(l, c, 1)
            wk.items = wk_sets['m3']
            K.phase_switch()
            ur = [[K.mkA("u%d_%d" % (j, tb)) for tb in range(2)] for j in range(2)]
            obr = [[K.mkA("ob%d_%d" % (j, tb)) for tb in range(2)] for j in range(2)]
            bgr = [K.mkA("bg%d" % j) for j in range(2)]
            ocr = [K.mkA("oc%d" % j) for j in range(2)]
            zr = [K.mkA("z%d" % j) for j in range(2)]
            ycr = [K.mkA("yc%d" % j) for j in range(2)]
            K.op(pool, lambda e: e.memset(zpad, 0.0), writes=zr)

            slot, sres = ring_get(4096)
            wD = slot.rearrange("p (k n) -> p k n", k=8)
            for tb in range(2):
                hs = [hT[:, k, sl(tb)] for k in range(8)]
                hres = [hr[k][tb] for k in range(8)]
                for j in range(2):
                    b = wk.next()
                    K.op(pe, MMG(psb[b], [wD[:, k, j * 128:(j + 1) * 128] for k in range(8)], hs), reads=[sres] + hres, writes=[PS[b]])
                    cgs, cgr = cgsR.next()
                    K.op(act, ACTF(cgs, psb[b], AF.Copy), reads=[PS[b]], writes=list(cgr))
                    b2 = wk.next()
                    K.op(pe, MMG(psb[b2], [wD[:, k, 256 + j * 128:256 + (j + 1) * 128] for k in range(8)], hs), reads=[sres] + hres, writes=[PS[b2]])
                    if is_s:
                        zdst = zpad[:, j, tb * 514 + 1:tb * 514 + 513]
                        K.op(dve, TT(zdst, psb[b2], cgs, ALU.mult), reads=[PS[b2]] + list(cgr), writes=[zr[j]])
                    else:
                        zv = zpad[:, j, 0:1032].rearrange("p (s q) -> p s q", s=4)
                        zdst = zv[:, 2 * tb:2 * tb + 2, 1:257]
                        K.op(dve, TT(zdst, psb[b2].rearrange("p (s q) -> p s q", s=2), cgs.rearrange("p (s q) -> p s q", s=2), ALU.mult),
                             reads=[PS[b2]] + list(cgr), writes=[zr[j]])
            ring_load()
            nsq = NSEQ * nbq // 2
            for j in range(2):
                zv = zpad[:, j, 0:NSEQ * (L + 2)].rearrange("p (s q) -> p s q", s=NSEQ)
                yv = ycv[:, j, :].rearrange("p (s q) -> p s q", s=NSEQ)
                if is_s:
                    me = vecs[:, V_MASK:V_MASK + 1]
                    mo = vecs[:, V_MASK + 1:V_MASK + 2]
                    for (db, dc, sb_, sc_, mk) in ((0, 0, 1, 512, mo), (0, 513, 1, 1, me), (1, 0, 0, 512, me), (1, 513, 0, 1, mo)):
                        K.op(dve, TS1(zv[:, db, dc:dc + 1], zv[:, sb_, sc_:sc_ + 1], mk, ALU.mult), reads=[zr[j], vecr], writes=[zr[j]])
                wcs = [vecs[:, V_WC + l * 6 + tap * 2 + j:V_WC + l * 6 + tap * 2 + j + 1] for tap in range(3)]
                K.op(dve, TS1(yv[:, 0:nsq, :], zv[:, 0:nsq, 1:L + 1], wcs[1], ALU.mult), reads=[zr[j], vecr], writes=[ycr[j]])
                K.op(dve, STT(yv[:, 0:nsq, :], zv[:, 0:nsq, 0:L], wcs[0], yv[:, 0:nsq, :], ALU.mult, ALU.add), reads=[zr[j], vecr, ycr[j]], writes=[ycr[j]])
                K.op(dve, STT(yv[:, 0:nsq, :], zv[:, 0:nsq, 2:L + 2], wcs[2], yv[:, 0:nsq, :], ALU.mult, ALU.add), reads=[zr[j], vecr, ycr[j]], writes=[ycr[j]])

            slot, sres = ring_get(4096)
            wC = slot.rearrange("p (k n) -> p k n", k=8)
            for tb in range(nbq):
                hs = [hT[:, k, sl(tb)] for k in range(8)]
                hres = [hr[k][tb] for k in range(8)]
                for j in range(2):
                    b = wk.next()
                    K.op(pe, MMG(psb[b], [wC[:, k, j * 128:(j + 1) * 128] for k in range(8)], hs), reads=[sres] + hres, writes=[PS[b]])
                    K.op(act, ACTF(uT[:, j, sl(tb)], psb[b], GELU), reads=[PS[b]], writes=[ur[j][tb]])
                for j in range(2):
                    b = wk.next()
                    K.op(pe, MMG(psb[b], [wC[:, k, 256 + j * 128:256 + (j + 1) * 128] for k in range(8)], hs), reads=[sres] + hres, writes=[PS[b]])
                    K.op(dve, CP(bgT[:, j, sl(tb)], psb[b]), reads=[PS[b]], writes=[bgr[j]])
            ring_load()

            for j in range(2):
                K.op(pool, TT(ocT[:, j, 0:nbq * TBS], ycv[:, j, 0:nbq * TBS], bgT[:, j, 0:nbq * TBS], ALU.mult), reads=[ycr[j], bgr[j]], writes=[ocr[j]])
            for tci in range(4 * nbq):
                tb = tci // 4
                tok = slice(tci * 128, (tci + 1) * 128)
                for fc in range(2):
                    b = wk.next()
                    K.op(pe, (lambda b, tci, fc: lambda e: e.matmul(psb[b][:, 0:256], vn[:, tci, fc * 128:(fc + 1) * 128], wsT[:, fc * 256:(fc + 1) * 256], start=True, stop=True))(b, tci, fc),
                         reads=[vnr[tci], wsr], writes=[PS[b]])
                    tmp, tr = tmpR.next()
                    for hl in range(2):
                        pp = slice(hl * 64, (hl + 1) * 64)
                        K.op(dve, STT(tmp[pp, 0:128], psb[b][pp, hl * 128:(hl + 1) * 128], vecs[pp, V_GV + l * 2 + fc:V_GV + l * 2 + fc + 1],
                                      bsb[pp, fc, :], ALU.mult, ALU.add), reads=[PS[b], vecr, bsr], writes=[tr])
                    for hl in range(2):
                        pp = slice(hl * 64, (hl + 1) * 64)
                        K.op(pool, TT(obT[pp, fc, tok], tmp[pp, 0:128], uT[pp, fc, tok], ALU.mult), reads=[tr, ur[fc][tb]], writes=[obr[fc][tb]])

            chk('conv')
            slot0, sres0 = ring_get(4096)
            slot1, sres1 = ring_get(4096)
            wO = [slot0.rearrange("p (k n) -> p k n", k=8), slot1.rearrange("p (k n) -> p k n", k=8)]
            parkw = [big, big2]
            parkwres = [[[bigr[m]] for m in range(8)], [[hr[m][0], hr[m][1]] for m in range(8)]]
            for tb in range(nbq):
                rhs = [qTn[:, h, sl(tb)] for h in range(4)] + [obT[:, j, sl(tb)] for j in range(2)] + [ocT[:, j, sl(tb)] for j in range(2)]
                rres = [oar[h][2 * tb] for h in range(4)] + [oar[h][2 * tb + 1] for h in range(4)] + [obr[j][tb] for j in range(2)] + ocr
                pend = []
                for m in range(8):
                    b = wk.next()
                    w = wO[m // 4]
                    K.op(pe, MMG(psb[b], [w[:, k, (m % 4) * 128:(m % 4 + 1) * 128] for k in range(8)], rhs),
                         reads=[sres0, sres1] + rres, writes=[PS[b]])
                    flush_pend(pend, 4 + tb)
                    K.op(act, ACTF(parkw[tb][:, m, :], psb[b], AF.Identity, scale=DER[:, 1, m:m + 1]), reads=[PS[b], derr[0]], writes=parkwres[tb][m])
                    sq, sqr = sqR.next()
                    K.op(act, ACTF(sq, psb[b], AF.Square), reads=[PS[b]], writes=[sqr])
                    pend.append((sq, sqr, m == 0, m == 7))
                flush_pend(pend, 4 + tb)
                rstd, rr = emit_rstd_bank(4 + tb, D)
                emit_residual(parkw[tb], parkwres[tb], 1, rstd, rr, tb)
            ring_load()
            ring_load()
            chk('wout')

            wk.items = wk_sets['f']
            K.phase_switch()
            f1r = [[K.mkA("f1_%d_%d" % (m, tb)) for tb in range(2)] for m in range(32)]
            do_mod_next = (not is_s) and (l + 1 < nlayers)
            emit_prenorm(l, c, 2, 24, 0)
            emit_warm(WARM_A)
            for j in range(8):
                slot, sres = ring_get(4096)
                w1 = slot.rearrange("p (k n) -> p k n", k=8)
                for tb in range(nbq):
                    if j == 0 and tb == 1:
                        emit_prenorm(l, c, 2, 24, 1)
                        emit_warm(WARM_B)
                    hs = [hT[:, k, sl(tb)] for k in range(8)]
                    hres = [hr[k][tb] for k in range(8)]
                    for mi in range(4):
                        m = 4 * j + mi
                        b = wk.next()
                        K.op(pe, MMG(psb[b], [w1[:, k, mi * 128:(mi + 1) * 128] for k in range(8)], hs), reads=[sres] + hres, writes=[PS[b]])
                        tmp, tr = tmpR.next()
                        K.op(act, ACTF(tmp, psb[b], AF.Relu), reads=[PS[b]], writes=[tr])
                        K.op(dve, TT(f1T[:, m, sl(tb)], tmp, tmp, ALU.mult), reads=[tr], writes=[f1r[m][tb]])
                ring_load()
                if do_mod_next:
                    emit_mod_piece(l + 1, j)
                    if j == 5:
                        emit_mod_evac(l + 1, 0)
            chk('ff1')
            park = [big, big2]
            parkres = [[[bigr[m]] for m in range(8)], [[hr[m][0], hr[m][1]] for m in range(8)]]
            pend = [[], []]

            def ff2_group(m, tb, w2, sres):
                b = wk.next()
                K.op(pe, MMG(psb[b], [w2[:, k, :] for k in range(32)], [f1T[:, k, sl(tb)] for k in range(32)]),
                     reads=[sres] + [f1r[k][tb] for k in range(32)], writes=[PS[b]])
                flush_pend(pend[tb], 4 + tb)
                K.op(act, ACTF(park[tb][:, m, :], psb[b], AF.Identity, scale=DER[:, 3, m:m + 1]), reads=[PS[b], derr[1]], writes=parkres[tb][m])
                sq, sqr = sqR.next()
                K.op(act, ACTF(sq, psb[b], AF.Square), reads=[PS[b]], writes=[sqr])
                pend[tb].append((sq, sqr, m == 0, m == 7))

            for m in range(6):
                slot, sres = ring_get(4096)
                w2 = slot.rearrange("p (k n) -> p k n", k=32)
                for tb in range(nbq):
                    ff2_group(m, tb, w2, sres)
                ring_load()
                if do_mod_next and m < 4:
                    emit_mod_piece(l + 1, 8 + m)
                    if m == 3:
                        emit_mod_evac(l + 1, 1)
            slot6, sres6 = ring_get(4096)
            slot7, sres7 = ring_get(4096)
            w26 = slot6.rearrange("p (k n) -> p k n", k=32)
            w27 = slot7.rearrange("p (k n) -> p k n", k=32)
            for tb in range(nbq):
                ff2_group(6, tb, w26, sres6)
                ff2_group(7, tb, w27, sres7)
                flush_pend(pend[tb], 4 + tb)
                rstd, rr = emit_rstd_bank(4 + tb, D)
                emit_residual(park[tb], parkres[tb], 3, rstd, rr, tb)
            ring_load()
            ring_load()
            K.phase_switch()
            chk('layer%d' % l)

        ncols = TBS if is_s else T
        for k in range(8):
            K.dma(sp, out=y_d[k * 128:(k + 1) * 128, :], in_=xT[:, k, 0:ncols], reads=[xr[k][0], xr[k][1]])

    try:
        emit_pass(False, xp_d, yp_d)
        if do_s:
            emit_pass(True, xs_d, ys_d)
    except _Stop:
        pass

    allres = [r for row in xr for r in row] + stR
    done = {}
    for r in allres:
        if r.dsem is not None:
            done[r.dsem] = max(done.get(r.dsem, 0), r.dcnt)
    for semidx, val in done.items():
        sp.eng.wait_ge(K.sems[semidx], val)
    return nc


def _fm(v):
    return np.ascontiguousarray(v.reshape(-1, 128).T)


def _piece_kn(w, c0, ncols):
    Kd = w.shape[0]
    blk = w[:, c0:c0 + ncols].reshape(Kd // 128, 128, ncols).transpose(1, 0, 2).reshape(128, -1)
    out = np.zeros((128, 4096), np.float32)
    out[:, :blk.shape[1]] = blk
    return out


def _rope_tables():
    rows = T // 64
    row = np.repeat(np.arange(rows, dtype=np.float32), 64)
    col = np.tile(np.arange(64, dtype=np.float32), rows)
    nf = 16
    inv = (np.float32(10000.0) ** (-np.arange(nf, dtype=np.float32) / np.float32(nf))).astype(np.float32)
    ang_r = (row[:, None] * inv).astype(np.float32)
    ang_c = (col[:, None] * inv).astype(np.float32)
    cr, sr, cc, sc = np.cos(ang_r), np.sin(ang_r), np.cos(ang_c), np.sin(ang_c)
    C = np.concatenate([cr, cr, cc, cc], axis=1)
    S = np.concatenate([-sr, sr, -sc, sc], axis=1)
    return np.ascontiguousarray(C.T.astype(np.float32)), np.ascontiguousarray(S.T.astype(np.float32))


_PROG = {}


def kernel(x_prompt, x_sample, cache_ckv, cache_krope, c, c_ctx,
           w_ada, b_ada, g_pre_mix, w_in, g_q, w_uq, g_kv, w_ukv,
           g_v, w_s, b_s, w_conv, w_out, g_post_mix,
           g_pre_ffn, w_ff1, w_ff2, g_post_ffn):
    f = lambda a: np.asarray(a, dtype=np.float32)
    x_prompt, x_sample, cache_ckv, cache_krope, c, c_ctx = map(f, (x_prompt, x_sample, cache_ckv, cache_krope, c, c_ctx))
    w_ada, b_ada, g_pre_mix, w_in, g_q, w_uq, g_kv, w_ukv = map(f, (w_ada, b_ada, g_pre_mix, w_in, g_q, w_uq, g_kv, w_ukv))
    g_v, w_s, b_s, w_conv, w_out, g_post_mix, g_pre_ffn, w_ff1, w_ff2, g_post_ffn = map(
        f, (g_v, w_s, b_s, w_conv, w_out, g_post_mix, g_pre_ffn, w_ff1, w_ff2, g_post_ffn))
    NC = 8
    wp = np.zeros((DEPTH * PIECES_PER_LAYER, 128, 4096), np.float32)
    for l in range(DEPTH):
        b = l * PIECES_PER_LAYER
        for j in range(12):
            wp[b + j] = _piece_kn(w_ada[l], j * 512, 512)
        wi = w_in[l]
        kr = wi[:, 640:704]
        A = np.concatenate([wi[:, 0:384], kr, kr[:, ROPE_PERM]], axis=1)
        B = np.concatenate([wi[:, 384:640], wi[:, 960:1216]], axis=1)
        Cc = np.concatenate([wi[:, 704:960], wi[:, 1216:1472]], axis=1)
        Dd = np.concatenate([wi[:, 1472:1728], wi[:, 1728:1984]], axis=1)
        wp[b + 12] = _piece_kn(A, 0, 512)
        wp[b + 13] = _piece_kn(B, 0, 512)
        wp[b + 14] = _piece_kn(w_ukv[l].reshape(256, 1024), 0, 1024)
        uq = w_uq[l]
        uqx = np.concatenate([uq[:, :, 0:128], uq[:, :, 128:192], uq[:, :, 128:192][:, :, ROPE_PERM]], axis=2).reshape(384, 1024)
        wp[b + 15] = _piece_kn(uqx, 0, 1024)
        wp[b + 16] = _piece_kn(Cc, 0, 512)
        wp[b + 17] = _piece_kn(Dd, 0, 512)
        wp[b + 18] = _piece_kn(w_out[l], 0, 512)
        wp[b + 19] = _piece_kn(w_out[l], 512, 512)
        for j in range(8):
            wp[b + 20 + j] = _piece_kn(w_ff1[l], j * 512, 512)
        for m in range(8):
            wp[b + 28 + m] = _piece_kn(w_ff2[l], m * 128, 128)
    vecs0 = np.zeros((128, NV), np.float32)
    for l in range(DEPTH):
        vecs0[:, V_BADA + l * 48:V_BADA + (l + 1) * 48] = _fm(b_ada[l])
        vecs0[:, V_GPM + l * 8:V_GPM + (l + 1) * 8] = _fm(g_pre_mix[l])
        vecs0[:, V_GPOM + l * 8:V_GPOM + (l + 1) * 8] = _fm(g_post_mix[l])
        vecs0[:, V_GPF + l * 8:V_GPF + (l + 1) * 8] = _fm(g_pre_ffn[l])
        vecs0[:, V_GPOF + l * 8:V_GPOF + (l + 1) * 8] = _fm(g_post_ffn[l])
        vecs0[:, V_GQ + l * 3:V_GQ + (l + 1) * 3] = _fm(g_q[l])
        vecs0[:, V_GKV + l * 2:V_GKV + (l + 1) * 2] = _fm(g_kv[l])
        vecs0[:, V_GV + l * 2:V_GV + (l + 1) * 2] = _fm(g_v[l])
        for tap in range(3):
            vecs0[:, V_WC + l * 6 + tap * 2:V_WC + l * 6 + tap * 2 + 2] = _fm(w_conv[l, tap])
    vecs0[:, V_COND:V_COND + 8] = _fm(c_ctx)
    wst = np.ascontiguousarray(w_s.transpose(0, 3, 1, 2).reshape(DEPTH, 128, 512))
    bsbh = np.zeros((DEPTH, 128, 2, 128), np.float32)
    for fc in range(2):
        for hl in range(2):
            bsbh[:, hl * 64:(hl + 1) * 64, fc, :] = b_s[:, fc * 2 + hl, None, :]
    bsbh = bsbh.reshape(DEPTH, 128, 256)
    ct, st = _rope_tables()

    in_maps = []
    for core in range(NC):
        bidx = core // 2
        v = vecs0.copy()
        v[:, V_COND + 8:V_COND + 16] = _fm(c[bidx])
        half = core % 2
        v[:, V_MASK] = 1.0 if half == 0 else 0.0
        v[:, V_MASK + 1] = 1.0 if half == 1 else 0.0
        perm = np.arange(T) if half == 0 else np.concatenate([np.arange(TBS, T), np.arange(0, TBS)])
        in_maps.append({
            "xp": np.ascontiguousarray(x_prompt[4 * core:4 * core + 4].reshape(T, D).T),
            "xs": np.ascontiguousarray(x_sample[bidx].T[:, perm]),
            "cckv": np.ascontiguousarray(cache_ckv[bidx].transpose(0, 2, 1)),
            "ckr": np.ascontiguousarray(cache_krope[bidx].transpose(0, 2, 1)),
            "vecs": v, "wp": wp, "ct": np.ascontiguousarray(ct[:, perm]), "st": np.ascontiguousarray(st[:, perm]), "wst": wst, "bsb": bsbh,
        })
    if "nc" not in _PROG:
        _PROG["nc"] = build_program()
    res = run_bass_kernel_spmd(_PROG["nc"], in_maps, core_ids=list(range(NC)))
    R = res.results
    y_prompt = np.zeros((32, 256, D), np.float32)
    y_sample = np.zeros((4, 1024, D), np.float32)
    new_ckv = np.zeros((32, DEPTH, 256, 256), np.float32)
    new_krope = np.zeros((32, DEPTH, 256, 64), np.float32)
    for core in range(NC):
        r = R[core]
        y_prompt[4 * core:4 * core + 4] = np.asarray(r["yp"]).T.reshape(4, 256, D)
        hf = core % 2
        y_sample[core // 2, hf * TBS:(hf + 1) * TBS] = np.asarray(r["ys"]).T
        ok = np.asarray(r["ockv"])
        new_ckv[4 * core:4 * core + 4] = ok.reshape(DEPTH, 256, 4, 256).transpose(2, 0, 3, 1)
        okr = np.asarray(r["okr"])
        new_krope[4 * core:4 * core + 4] = okr.reshape(DEPTH, 64, 4, 256).transpose(2, 0, 3, 1)
    return (y_prompt, y_sample, new_ckv, new_krope)
```
